# Optimizing a Trainium2 kernel written in Bass

```python
import math
import jax
import jax.numpy as jnp
from jax import lax
import numpy as np

D_MODEL = 1024
BATCH = 1
SEQ = 16384
DEPTH = 2
DEC_BATCH = 4
DEC_SEQ = 8192
PAST_LEN = 128

GRID_W = 64
HGRN_HEADS = 4
HGRN_KEY_DIM = 128
HGRN_VAL_DIM = 128
HGRN_WIDTH = HGRN_HEADS * HGRN_VAL_DIM
HGRN_CHUNK = 32
GQA_HEADS = 4
GQA_KV_HEADS = 2
GQA_GROUP = GQA_HEADS // GQA_KV_HEADS
GQA_HEAD_DIM = 64
GQA_WIDTH = GQA_HEADS * GQA_HEAD_DIM
DIFF_HEADS = 4
DIFF_HEAD_DIM = 32
DIFF_WIDTH = DIFF_HEADS * 2 * DIFF_HEAD_DIM
MIX_WIDTH = HGRN_WIDTH + GQA_WIDTH + DIFF_WIDTH
Q_BLOCK = 128
ROPE_THETA = 10000.0
NORM_EPS = 1e-6
SPLIT_SIZES = (
    HGRN_HEADS * HGRN_KEY_DIM,
    HGRN_HEADS * HGRN_KEY_DIM,
    HGRN_HEADS * HGRN_KEY_DIM,
    HGRN_WIDTH,
    HGRN_WIDTH,
    GQA_HEADS * GQA_HEAD_DIM,
    GQA_KV_HEADS * GQA_HEAD_DIM,
    GQA_KV_HEADS * GQA_HEAD_DIM,
    GQA_WIDTH,
    DIFF_HEADS * 2 * DIFF_HEAD_DIM,
    DIFF_HEADS * 2 * DIFF_HEAD_DIM,
    DIFF_WIDTH,
    DIFF_WIDTH,
)
IN_COLS = 4352

kernel_name = "hymba_hgrn2_axialgqa_diffattn_encoder"


def _rms(x, w):
    xf = x.astype(jnp.float32)
    y = xf * lax.rsqrt(jnp.mean(xf * xf, axis=-1, keepdims=True) + NORM_EPS)
    return (y * w.astype(jnp.float32)).astype(x.dtype)


def _rope_tables(pos, dim):
    inv = jnp.power(ROPE_THETA, -jnp.arange(0, dim, 2, dtype=jnp.float32) / dim)
    ang = pos[:, None] * inv[None, :]
    ang = jnp.concatenate([ang, ang], axis=-1)
    return jnp.cos(ang), jnp.sin(ang)


def _apply_rope(x, cos, sin):
    shape = (1, x.shape[1]) + (1,) * (x.ndim - 3) + (x.shape[-1],)
    c = cos.reshape(shape)
    s = sin.reshape(shape)
    xf = x.astype(jnp.float32)
    half = x.shape[-1] // 2
    rot = jnp.concatenate([-xf[..., half:], xf[..., :half]], axis=-1)
    return (xf * c + rot * s).astype(x.dtype)


def _gla_chunked(q, k, v, g):
    B, H, L, dk = q.shape
    dv = v.shape[-1]
    n = L // HGRN_CHUNK
    q = q.reshape(B, H, n, HGRN_CHUNK, dk)
    k = k.reshape(B, H, n, HGRN_CHUNK, dk)
    v = v.reshape(B, H, n, HGRN_CHUNK, dv)
    g = g.reshape(B, H, n, HGRN_CHUNK, dk)
    b = jnp.cumsum(g, axis=3)
    mid = HGRN_CHUNK // 2
    b_mid = b[:, :, :, mid:mid + 1]
    b_last = b[:, :, :, -1:]
    qm = q * jnp.exp(b - b_mid)
    km = k * jnp.exp(b_mid - b)
    a = jnp.einsum('bhntd,bhnsd->bhnts', qm, km)
    causal_in_chunk = jnp.tril(jnp.ones((HGRN_CHUNK, HGRN_CHUNK), dtype=bool))
    a = jnp.where(causal_in_chunk, a, 0.0)
    o_intra = jnp.einsum('bhnts,bhnse->bhnte', a, v)
    d_state = jnp.einsum('bhnsd,bhnse->bhnde', k * jnp.exp(b_last - b), v)
    chunk_decay = jnp.exp(b_last[:, :, :, 0, :])

    def step(S, inp):
        dec, ds = inp
        return dec[..., None] * S + ds, S

    S0 = jnp.zeros((B, H, dk, dv), dtype=jnp.float32)
    _, S_prev = lax.scan(step, S0, (jnp.moveaxis(chunk_decay, 2, 0), jnp.moveaxis(d_state, 2, 0)))
    S_prev = jnp.moveaxis(S_prev, 0, 2)
    o_inter = jnp.einsum('bhntd,bhnde->bhnte', q * jnp.exp(b), S_prev)
    return (o_intra + o_inter).reshape(B, H, L, dv)


def _hgrn2_branch(aq, af, afb, ai, lb, norm_w):
    B, L, _ = aq.shape

    def heads(t):
        return t.astype(jnp.float32).reshape(B, L, HGRN_HEADS, -1).transpose(0, 2, 1, 3)

    q = jax.nn.silu(heads(aq))
    v = heads(ai)
    out = None
    for direction, fx in enumerate((af, afb)):
        lbd = lb[direction].reshape(HGRN_HEADS, 1, HGRN_KEY_DIM)
        xf = heads(fx)
        g = jnp.logaddexp(jnp.log(lbd), jnp.log1p(-lbd) + jax.nn.log_sigmoid(xf))
        k = (1.0 - lbd) * jax.nn.sigmoid(-xf)
        if direction == 0:
            o = _gla_chunked(q, k, v, g)
        else:
            o = jnp.flip(_gla_chunked(jnp.flip(q, 2), jnp.flip(k, 2), jnp.flip(v, 2), jnp.flip(g, 2)), 2)
        out = o if out is None else out + o
    out = out.transpose(0, 2, 1, 3)
    out = _rms(out, norm_w.reshape(HGRN_HEADS, HGRN_VAL_DIM))
    return out.reshape(B, L, HGRN_WIDTH).astype(aq.dtype)


def _query_blocks(q):
    B, L = q.shape[:2]
    nb = L // Q_BLOCK
    return jnp.moveaxis(q.reshape((B, nb, Q_BLOCK) + q.shape[2:]), 1, 0)


def _unblock(o, B, L):
    o = jnp.moveaxis(o, 0, 1)
    return o.reshape((B, L) + o.shape[3:])


def _gqa_attention(q, k, v):
    B, L = q.shape[:2]
    scale = 1.0 / math.sqrt(GQA_HEAD_DIM)

    def blk(qi):
        s = jnp.einsum('bqkgd,bskd->bkgqs', qi, k).astype(jnp.float32) * scale
        p = jax.nn.softmax(s, axis=-1).astype(v.dtype)
        return jnp.einsum('bkgqs,bskd->bqkgd', p, v)

    o = lax.map(blk, _query_blocks(q))
    return _unblock(o, B, L)


def _diff_attention(q, k, v, lam):
    B, L = q.shape[:2]
    scale = 1.0 / math.sqrt(DIFF_HEAD_DIM)

    def blk(qi):
        s = jnp.einsum('bqhcd,bshcd->bhcqs', qi, k).astype(jnp.float32) * scale
        p = jax.nn.softmax(s, axis=-1)
        w = (p[:, :, 0] - lam * p[:, :, 1]).astype(v.dtype)
        return jnp.einsum('bhqs,bshe->bqhe', w, v)

    o = lax.map(blk, _query_blocks(q))
    return _unblock(o, B, L)


def _trunk(x, pre_norm_w, w_in, hgrn_lb, hgrn_norm_w, gqa_q_norm_w, gqa_k_norm_w,
           diff_lambda, diff_norm_w, w_out, post_norm_w):
    B, L, _ = x.shape
    rows = L // GRID_W
    pos = jnp.arange(L, dtype=jnp.float32)
    row_pos = jnp.repeat(jnp.arange(rows, dtype=jnp.float32), GRID_W)
    col_pos = jnp.tile(jnp.arange(GRID_W, dtype=jnp.float32), rows)
    half = GQA_HEAD_DIM // 2
    cos_r, sin_r = _rope_tables(row_pos, half)
    cos_c, sin_c = _rope_tables(col_pos, half)
    cos_1, sin_1 = _rope_tables(pos, DIFF_HEAD_DIM)

    lb_all = jnp.cumsum(jax.nn.softmax(hgrn_lb.astype(jnp.float32), axis=0), axis=0)
    lb_all = lb_all - lb_all[:1]

    split_idx = []
    acc = 0
    for s in SPLIT_SIZES[:-1]:
        acc += s
        split_idx.append(acc)

    def axial(t):
        return jnp.concatenate([_apply_rope(t[..., :half], cos_r, sin_r),
                                _apply_rope(t[..., half:], cos_c, sin_c)], axis=-1)

    for layer in range(DEPTH):
        h = _rms(x, pre_norm_w[layer])
        proj = jnp.einsum('bld,dc->blc', h, w_in[layer])
        (aq, af, afb, ai, ag, bq, bk, bv, bg, cq, ck, cv, cg) = jnp.split(proj, split_idx, axis=-1)

        a_out = _hgrn2_branch(aq, af, afb, ai, lb_all[layer], hgrn_norm_w[layer]) * jax.nn.silu(ag)

        q = _rms(bq.reshape(B, L, GQA_KV_HEADS, GQA_GROUP, GQA_HEAD_DIM), gqa_q_norm_w[layer])
        k = _rms(bk.reshape(B, L, GQA_KV_HEADS, GQA_HEAD_DIM), gqa_k_norm_w[layer])
        v = bv.reshape(B, L, GQA_KV_HEADS, GQA_HEAD_DIM)
        b_out = _gqa_attention(axial(q), axial(k), v).reshape(B, L, GQA_WIDTH) * jax.nn.silu(bg)

        lam_init = 0.8 - 0.6 * math.exp(-0.3 * layer)
        lp = diff_lambda[layer].astype(jnp.float32)
        lam = jnp.exp(jnp.sum(lp[0] * lp[1])) - jnp.exp(jnp.sum(lp[2] * lp[3])) + lam_init
        q = _apply_rope(cq.reshape(B, L, DIFF_HEADS, 2, DIFF_HEAD_DIM), cos_1, sin_1)
        k = _apply_rope(ck.reshape(B, L, DIFF_HEADS, 2, DIFF_HEAD_DIM), cos_1, sin_1)
        v = cv.reshape(B, L, DIFF_HEADS, 2 * DIFF_HEAD_DIM)
        o = _rms(_diff_attention(q, k, v, lam), diff_norm_w[layer]) * (1.0 - lam_init)
        c_out = o.reshape(B, L, DIFF_WIDTH) * jax.nn.silu(cg)

        mix = jnp.concatenate([a_out, b_out, c_out], axis=-1)
        y = jnp.einsum('blc,cd->bld', mix, w_out[layer])
        x = x + _rms(y, post_norm_w[layer])
    return x


def setup_inputs(seed: int = 0) -> dict:
    key = jax.random.key(seed)
    ks = jax.random.split(key, 13)
    f32 = jnp.float32
    return {
        "x_prompt": jax.random.normal(ks[0], (BATCH, SEQ, D_MODEL), f32),
        "x_sample": jax.random.normal(ks[1], (DEC_BATCH, DEC_SEQ, D_MODEL), f32),
        "pre_norm_w": 1.0 + 0.05 * jax.random.normal(ks[2], (DEPTH, D_MODEL), f32),
        "w_in": jax.random.normal(ks[3], (DEPTH, D_MODEL, IN_COLS), f32) * D_MODEL ** -0.5,
        "hgrn_lb": 0.1 * jax.random.normal(ks[4], (DEPTH, 2, HGRN_HEADS * HGRN_KEY_DIM), f32),
        "hgrn_norm_w": 1.0 + 0.05 * jax.random.normal(ks[5], (DEPTH, HGRN_WIDTH), f32),
        "gqa_q_norm_w": 1.0 + 0.05 * jax.random.normal(ks[6], (DEPTH, GQA_HEAD_DIM), f32),
        "gqa_k_norm_w": 1.0 + 0.05 * jax.random.normal(ks[7], (DEPTH, GQA_HEAD_DIM), f32),
        "diff_lambda": 0.1 * jax.random.normal(ks[8], (DEPTH, 4, DIFF_HEAD_DIM), f32),
        "diff_norm_w": 1.0 + 0.05 * jax.random.normal(ks[9], (DEPTH, 2 * DIFF_HEAD_DIM), f32),
        "w_out": jax.random.normal(ks[10], (DEPTH, MIX_WIDTH, D_MODEL), f32) * MIX_WIDTH ** -0.5,
        "post_norm_w": 1.0 + 0.05 * jax.random.normal(ks[11], (DEPTH, D_MODEL), f32),
    }


def reference(x_prompt, x_sample, pre_norm_w, w_in, hgrn_lb, hgrn_norm_w, gqa_q_norm_w,
              gqa_k_norm_w, diff_lambda, diff_norm_w, w_out, post_norm_w):
    y_prompt = _trunk(x_prompt, pre_norm_w, w_in, hgrn_lb, hgrn_norm_w, gqa_q_norm_w, gqa_k_norm_w,
                      diff_lambda, diff_norm_w, w_out, post_norm_w)
    y_sample = _trunk(x_sample, pre_norm_w, w_in, hgrn_lb, hgrn_norm_w, gqa_q_norm_w, gqa_k_norm_w,
                      diff_lambda, diff_norm_w, w_out, post_norm_w)
    return (y_prompt, y_sample)
```

```python
import math
import numpy as np
import ml_dtypes
import concourse.bass as bass
import concourse.mybir as mybir
from concourse.bass_utils import run_bass_kernel_spmd

F32 = mybir.dt.float32
BF16 = mybir.dt.bfloat16
ALU = mybir.AluOpType
AF = mybir.ActivationFunctionType
AX = mybir.AxisListType

D = 1024
DEPTH = 2
INC = 4352
EPS = 1e-6
NCORES = 8
CH = 32
DEBUG = False


class Op:
    __slots__ = ("idx", "eng", "fn", "deps", "dma", "lidx", "marked", "count", "epoch", "sem", "semval")

    def __init__(self, idx, eng, fn, dma):
        self.idx = idx; self.eng = eng; self.fn = fn; self.dma = dma
        self.deps = set(); self.marked = False; self.count = 0; self.epoch = 0
        self.sem = None; self.semval = 0; self.lidx = 0


class Sched:
    ENGS = ("pe", "act", "dve", "pool", "sp")
    NDS = 20
    EPOCH = 20000

    def __init__(self):
        self.ops = []
        self.lw = {}
        self.rd = {}
        self.per = {e: [] for e in self.ENGS}

    def add(self, eng, fn, reads=(), writes=(), dma=False):
        op = Op(len(self.ops), eng, fn, dma)
        for r in reads:
            if r in self.lw:
                op.deps.add(self.lw[r])
        for w in writes:
            if w in self.lw:
                op.deps.add(self.lw[w])
            for x in self.rd.get(w, ()):
                op.deps.add(x)
        for r in reads:
            self.rd.setdefault(r, []).append(op.idx)
        for w in writes:
            self.lw[w] = op.idx
            self.rd[w] = []
        op.deps.discard(op.idx)
        op.lidx = len(self.per[eng])
        self.per[eng].append(op)
        self.ops.append(op)
        return op

    def emit(self, nc, stack):
        ops = self.ops
        for op in ops:
            for d in op.deps:
                p = ops[d]
                if p.dma:
                    continue
                if p.eng == op.eng:
                    if p.eng == "pe":
                        continue
                    if op.dma or p.lidx >= op.lidx - 3:
                        p.marked = True
                else:
                    p.marked = True
        nep = {}
        for e in self.ENGS:
            c = 0
            for op in self.per[e]:
                if op.dma:
                    continue
                if op.marked:
                    c += 1
                    op.epoch = (c - 1) // self.EPOCH
                    op.count = (c - 1) % self.EPOCH + 1
            nep[e] = max(1, (c + self.EPOCH - 1) // self.EPOCH)
        csem = {e: [stack.enter_context(nc.semaphore(f"c_{e}_{i}")) for i in range(nep[e])] for e in self.ENGS}
        dsem = {e: [stack.enter_context(nc.semaphore(f"d_{e}_{i}")) for i in range(self.NDS)] for e in ("sp", "pool", "act")}
        dcount = {e: [0] * self.NDS for e in dsem}
        dprev = {e: [None] * self.NDS for e in dsem}
        for e in dsem:
            j = 0
            for op in self.per[e]:
                if op.dma:
                    k = j % self.NDS
                    dcount[e][k] += 1
                    op.sem = dsem[e][k]
                    op.semval = 16 * dcount[e][k]
                    op.count = dprev[e][k]
                    dprev[e][k] = op
                    j += 1
        block = stack.enter_context(nc.Block())
        final_waits = [op for op in ops if op.dma and op.fn is not None and getattr(op.fn, "_is_out", False)]

        def run_engine(e, eng):
            wm = {}

            def wait(sem, val):
                key = id(sem)
                if wm.get(key, 0) >= val:
                    return
                eng.wait_ge(sem, val)
                wm[key] = val

            for op in self.per[e]:
                for d in sorted(op.deps):
                    p = ops[d]
                    if p.dma:
                        wait(p.sem, p.semval)
                    elif p.eng == e:
                        if e != "pe" and p.marked and (op.dma or p.lidx >= op.lidx - 3):
                            wait(csem[e][p.epoch], p.count)
                    else:
                        wait(csem[p.eng][p.epoch], p.count)
                if op.dma:
                    if op.count is not None:
                        wait(op.count.sem, op.count.semval)
                    ins = op.fn(eng)
                    ins.then_inc(op.sem, 16)
                else:
                    ins = op.fn(eng)
                    if op.marked:
                        ins.then_inc(csem[e][op.epoch], 1)
            if e == "sp":
                for op in final_waits:
                    wait(op.sem, op.semval)

        @block.sync
        def _(eng):
            run_engine("sp", eng)

        @block.gpsimd
        def _(eng):
            run_engine("pool", eng)

        @block.tensor
        def _(eng):
            run_engine("pe", eng)

        @block.scalar
        def _(eng):
            run_engine("act", eng)

        @block.vector
        def _(eng):
            run_engine("dve", eng)


def _rope_tables(pos, dim):
    inv = np.power(10000.0, -np.arange(0, dim, 2, dtype=np.float32) / dim).astype(np.float32)
    ang = pos[:, None].astype(np.float32) * inv[None, :]
    ang = np.concatenate([ang, ang], axis=-1)
    c = np.cos(ang).astype(np.float32)
    s = np.sin(ang).astype(np.float32)
    half = dim // 2
    s[:, :half] *= -1.0
    return c, s


def make_consts(T):
    pos = np.arange(T, dtype=np.float32)
    row = np.floor(pos / 64.0).astype(np.float32)
    col = (pos - 64.0 * row).astype(np.float32)
    cr, sr = _rope_tables(row, 32)
    cc, sc = _rope_tables(col, 32)
    cosB = np.concatenate([cr, cc], -1)
    sinB = np.concatenate([sr, sc], -1)
    cosC, sinC = _rope_tables(pos, 32)
    p = np.arange(128)
    ch = p // CH
    ps = p % CH
    same = (ch[:, None] == ch[None, :]).astype(np.float32)
    s_idx = p[:, None]
    t_idx = p[None, :]
    mats = {}
    for d in (0, 1):
        if d == 0:
            tri = same * (s_idx <= t_idx)
            midm = same * (ps[:, None] <= 16)
        else:
            tri = same * (s_idx >= t_idx)
            midm = same * (ps[:, None] >= 15)
        mats[d] = dict(tri=tri.astype(np.float32), d1=(tri - midm).astype(np.float32),
                       d3=(same - tri).astype(np.float32), am=tri.astype(np.float32))
    sel = (ch[:, None] == np.arange(4)[None, :]).astype(np.float32)
    hm = np.stack([mats[0]["tri"], mats[0]["d1"], mats[0]["d3"], mats[1]["tri"], mats[1]["d1"], mats[1]["d3"]], 0)
    am = np.stack([mats[0]["am"], mats[1]["am"]], 0)
    return dict(cosB=cosB, sinB=sinB, cosC=cosC, sinC=sinC, hm=hm.astype(np.float32), am=am.astype(np.float32),
                sel=sel, ident=np.eye(128, dtype=np.float32))


def build(T):
    NT = T // 128
    nc = bass.Bass("TRN2", target_bir_lowering=False)
    S = Sched()

    def din(name, shape, dt=F32):
        return nc.dram_tensor(name, list(shape), dt, kind="ExternalInput").ap()

    x_in = din("x", [T, D])
    w_in_d = din("w_in", [DEPTH, D, INC])
    w_out_d = din("w_out", [DEPTH, D, D])
    pre_w = din("pre_norm_w", [DEPTH, D])
    post_w = din("post_norm_w", [DEPTH, D])
    hlb = din("hgrn_lb", [DEPTH, 2, 512])
    hnw = din("hgrn_norm_w", [DEPTH, 512])
    qnw = din("gqa_q_norm_w", [DEPTH, 64])
    knw = din("gqa_k_norm_w", [DEPTH, 64])
    dlam = din("diff_lambda", [DEPTH, 4, 32])
    dnw = din("diff_norm_w", [DEPTH, 64])
    cosB_d = din("cosB", [T, 64]); sinB_d = din("sinB", [T, 64])
    cosC_d = din("cosC", [T, 32]); sinC_d = din("sinC", [T, 32])
    hm_d = din("hm", [6, 128, 128]); am_d = din("am", [2, 128, 128])
    sel_d = din("sel", [128, 4]); ident_d = din("ident", [128, 128])
    tmask_d = din("tmask", [128, NT])
    y_out = nc.dram_tensor("y", [T, D], F32, kind="ExternalOutput").ap()

    def dscr(name, shape, dt):
        if DEBUG:
            return nc.dram_tensor(name, list(shape), dt, kind="ExternalOutput").ap()
        return nc.dram_tensor(name, list(shape), dt).ap()

    x1 = dscr("x1", [T, D], F32)
    QT = dscr("QT", [4, 128, T], BF16)
    KT = dscr("KT", [3, 128, T], BF16)
    VA = dscr("VA", [3, 128, NT, 130], BF16)
    HQ = dscr("HQ", [T, 512], BF16)
    HV = dscr("HV", [T, 512], BF16)
    HG = dscr("HG", [2, T, 512], F32)
    HK = dscr("HK", [2, T, 512], F32)
    GATE = dscr("GATE", [T, 1024], F32)
    OH = dscr("OH", [T, 512], F32)
    AO = dscr("AO", [T, 12, 64], F32)
    MIX = dscr("MIX", [T, 1024], BF16)

    import contextlib
    stack = contextlib.ExitStack()
    with stack:
        def sb(name, shape, dt=F32):
            return stack.enter_context(nc.sbuf_tensor("s_" + name, list(shape), dt))

        def pst(name, shape, dt=F32):
            return stack.enter_context(nc.psum_tensor("p_" + name, list(shape), dt))

        ident = sb("ident", [128, 128]); identb = sb("identb", [128, 128], BF16)
        hm = sb("hm", [128, 6, 128]); am = sb("am", [128, 2, 128])
        sel = sb("sel", [128, 4]); tmask = sb("tmask", [128, NT])
        preB1 = sb("preB", [128, 1, D]); postB1 = sb("postB", [128, 1, D])
        hnB = sb("hnB", [128, DEPTH, 512]); qnB = sb("qnB", [128, DEPTH, 64]); knB = sb("knB", [128, DEPTH, 64])
        dnB = sb("dnB", [128, DEPTH, 64]); lamB = sb("lamB", [128, DEPTH, 128])
        lb1 = sb("lb1", [128, 2, 512]); oml1 = sb("oml1", [128, 2, 512])
        lamv = sb("lamv", [128, DEPTH, 4]); lamt = sb("lamt", [128, DEPTH, 128])
        eps_t = sb("eps_t", [128, 1]); one_t = sb("one_t", [128, 1])

        def bc(ap2d, n):
            return ap2d.partition_broadcast(128)

        def dma(eng, out, in_, reads, writes, is_out=False):
            def fn(e, out=out, in_=in_):
                return e.dma_start(out=out, in_=in_)
            fn._is_out = is_out
            return S.add(eng, fn, reads, writes, dma=True)

        dma("sp", ident[:], ident_d[:, :], [], ["ident"])
        dma("sp", hm[:], hm_d.rearrange("m p q -> p m q"), [], ["hm"])
        dma("sp", am[:], am_d.rearrange("m p q -> p m q"), [], ["am"])
        dma("sp", sel[:], sel_d[:, :], [], ["sel"])
        dma("sp", tmask[:], tmask_d[:, :], [], ["tmask"])
        for l in range(DEPTH):
            dma("sp", hnB[:, l, :], hnw[l:l + 1, :].partition_broadcast(128), [], [("hnB", l)])
            dma("sp", qnB[:, l, :], qnw[l:l + 1, :].partition_broadcast(128), [], [("qnB", l)])
            dma("sp", knB[:, l, :], knw[l:l + 1, :].partition_broadcast(128), [], [("knB", l)])
            dma("sp", dnB[:, l, :], dnw[l:l + 1, :].partition_broadcast(128), [], [("dnB", l)])
            dma("sp", lamB[:, l, :], dlam[l:l + 1, :, :].rearrange("o a b -> o (a b)").partition_broadcast(128), [], [("lamB", l)])
        S.add("dve", lambda e: e.memset(eps_t[:], EPS), [], ["eps"])
        S.add("dve", lambda e: e.memset(one_t[:], 1.0), [], ["one"])
        S.add("dve", lambda e: e.tensor_copy(out=identb[:], in_=ident[:]), ["ident"], ["identb"])
        for l in range(DEPTH):
            lam_init = 0.8 - 0.6 * math.exp(-0.3 * l)
            S.add("dve", lambda e, l=l: e.tensor_tensor(out=lamt[:, l, 0:32], in0=lamB[:, l, 0:32], in1=lamB[:, l, 32:64], op=ALU.mult), [("lamB", l)], [("lamt", l, 0)])
            S.add("dve", lambda e, l=l: e.tensor_tensor(out=lamt[:, l, 32:64], in0=lamB[:, l, 64:96], in1=lamB[:, l, 96:128], op=ALU.mult), [("lamB", l)], [("lamt", l, 1)])
            S.add("dve", lambda e, l=l: e.reduce_sum(out=lamv[:, l, 0:2], in_=lamt[:, l, 0:64].rearrange("p (a b) -> p a b", a=2), axis=AX.X),
                  [("lamt", l, 0), ("lamt", l, 1)], [("lamv", l)])
            S.add("act", lambda e, l=l: e.activation(out=lamv[:, l, 0:2], in_=lamv[:, l, 0:2], func=AF.Exp), [("lamv", l)], [("lamv", l)])
            S.add("dve", lambda e, l=l: e.tensor_sub(out=lamv[:, l, 2:3], in0=lamv[:, l, 0:1], in1=lamv[:, l, 1:2]), [("lamv", l)], [("lamv2", l)])
            S.add("dve", lambda e, l=l, li=lam_init: e.tensor_scalar(out=lamv[:, l, 3:4], in0=lamv[:, l, 2:3], scalar1=li, scalar2=-1.0, op0=ALU.add, op1=ALU.mult),
                  [("lamv2", l)], [("nlam", l)])

        wsb = sb("wsb", [128, 8, INC], BF16)
        wosb = sb("wosb", [128, 8, D], BF16)
        xt = sb("xt", [128, D]); sq = sb("sq", [128, D]); hb = sb("hb", [128, D], BF16)
        hT = sb("hT", [128, 8, 128], BF16)
        st = sb("st", [128, 8])
        pr = sb("pr", [128, INC])
        lbB = pr[:, 0:2048].rearrange("p (a n) -> p a n", a=4)
        for l_ in range(DEPTH):
            for d in range(2):
                dma("sp", lbB[:, l_ * 2 + d, :], hlb[l_, d:d + 1, :].partition_broadcast(128), [], [("lbB", l_, d), ("pr", l_ * 2 + d)])
        for d in range(2):
            S.add("dve", lambda e, d=d: e.tensor_sub(out=lb1[:, d, :], in0=lbB[:, d, :], in1=lbB[:, 2 + d, :]),
                  [("lbB", 0, d), ("lbB", 1, d), ("pr", d), ("pr", 2 + d)], [("lb1", d)])
            S.add("act", lambda e, d=d: e.activation(out=lb1[:, d, :], in_=lb1[:, d, :], func=AF.Exp), [("lb1", d)], [("lb1", d)])
            S.add("dve", lambda e, d=d: e.tensor_scalar_add(out=lb1[:, d, :], in0=lb1[:, d, :], scalar1=1.0), [("lb1", d)], [("lb1", d)])
            S.add("dve", lambda e, d=d: e.reciprocal(out=lb1[:, d, :], in_=lb1[:, d, :]), [("lb1", d)], [("lb1", d)])
            S.add("dve", lambda e, d=d: e.tensor_scalar(out=oml1[:, d, :], in0=lb1[:, d, :], scalar1=-1.0, scalar2=1.0, op0=ALU.mult, op1=ALU.add),
                  [("lb1", d)], [("oml1", d)])
        tA = sb("tA", [128, 512]); tB = sb("tB", [128, 512]); tC = sb("tC", [128, 512]); tD = sb("tD", [128, 512])
        gsb = sb("gsb", [128, 1024])
        ropeq = sb("ropeq", [128, 1024]); ropeb = sb("ropeb", [128, 1024], BF16)
        cB = sb("cB", [128, 64]); sB = sb("sB", [128, 64]); cC = sb("cC", [128, 32]); sC = sb("sC", [128, 32])
        tq = sb("tq", [128, 512], BF16)
        vaug = sb("vaug", [128, 3, 130], BF16)
        qkT = sb("qkT", [128, 7, 128], BF16)
        hq = sb("hq", [128, 512], BF16); hv = sb("hv", [128, 512], BF16); hg = sb("hg", [128, 512]); hk = sb("hk", [128, 512])
        qmb = sb("qmb", [128, 512], BF16); kmb = sb("kmb", [128, 512], BF16); qbb = sb("qbb", [128, 512], BF16)
        kdm = sb("kdm", [128, 4, 512], BF16); dec = sb("dec", [128, 16])
        qkmT = sb("qkmT", [128, 1024], BF16); qbT = sb("qbT", [128, 512], BF16); ATb = sb("ATb", [128, 512], BF16)
        S32 = sb("S32", [128, 512]); S16 = sb("S16", [128, 512], BF16)
        oTs = sb("oTs", [128, 512]); otok = sb("otok", [128, 512]); mixa = sb("mixa", [128, 512], BF16)
        qs = sb("qs", [128, min(1024, T)], BF16); pT = sb("pT", [128, 2, 1024], BF16)
        oas = sb("oas", [65, 1024]); rz = sb("rz", [128, 8]); aos = sb("aos", [128, 8, 64])
        aot = sb("aot", [128, 768]); mixb = sb("mixb", [128, 1024], BF16)
        wflat = wsb[:].rearrange("p c n -> p (c n)")
        KTs = wflat[:, 0:T]
        VAs = wflat[:, T:T + NT * 130].rearrange("p (t c) -> p t c", c=130)
        psA = pst("psA", [128, 2048])
        psB = pst("psB", [128, 2048])

        def rstd_from_ss(ss_ap, n, key_r, key_w):
            S.add("act", lambda e: e.activation(out=ss_ap, in_=ss_ap, func=AF.Ln, scale=1.0 / n, bias=eps_t[:, 0:1]), [key_r, "eps"], [key_w])
            S.add("act", lambda e: e.activation(out=ss_ap, in_=ss_ap, func=AF.Exp, scale=-0.5), [key_w], [key_w])

        def transpose_to(dst_ap, src_ap, ps_ap, rkeys, wkeys, pskey, bf=True, evac="dve"):
            idn = identb if bf else ident
            S.add("pe", lambda e: e.transpose(ps_ap, src_ap, idn[:]), list(rkeys) + ["identb" if bf else "ident"], [pskey])
            S.add(evac, (lambda e: e.tensor_copy(out=dst_ap, in_=ps_ap)) if evac != "act" else (lambda e: e.copy(out=dst_ap, in_=ps_ap)), [pskey], list(wkeys))

        psAb = psA[:].bitcast(BF16) if False else None

        for l in range(DEPTH):
            dma("sp", preB1[:, 0, :], pre_w[l:l + 1, :].partition_broadcast(128), [], ["preB"])
            dma("sp", postB1[:, 0, :], post_w[l:l + 1, :].partition_broadcast(128), [], ["postB"])
            xin = x_in if l == 0 else x1
            xout = x1 if l == 0 else y_out
            xin_key = None
            for c in range(8):
                dma("pool", wsb[:, c, :], w_in_d[l, c * 128:(c + 1) * 128, :], [], [("wsb", c), "KTs", "VAs"])
                dma("pool", wosb[:, c, :], w_out_d[l, c * 128:(c + 1) * 128, :], [], [("wosb", c)])

            for t in range(NT):
                r0 = t * 128
                dma("sp", xt[:], xin[r0:r0 + 128, :], ([("x1", t)] if l == 1 else []), ["xt"])
                S.add("act", lambda e: e.activation(out=sq[:], in_=xt[:], func=AF.Square, accum_out=st[:, 0:1]), ["xt"], ["sq", "st"])
                rstd_from_ss(st[:, 0:1], D, "st", "st")
                S.add("dve", lambda e, l=l: e.scalar_tensor_tensor(out=hb[:], in0=xt[:], scalar=st[:, 0:1], in1=preB1[:, 0, :], op0=ALU.mult, op1=ALU.mult),
                      ["xt", "st", "preB"], ["hb"])
                psT = psA[:, 0:512].bitcast(BF16)
                for c in range(8):
                    S.add("pe", lambda e, c=c: e.transpose(psT[:, c * 128:(c + 1) * 128], hb[:, c * 128:(c + 1) * 128], identb[:]), ["hb", "identb"], [("psA", 0)])
                S.add("dve", lambda e: e.tensor_copy(out=hT[:].rearrange("p c t -> p (c t)"), in_=psT[:, 0:1024]), [("psA", 0)], ["hT"])
                for g in range(9):
                    c0 = g * 512
                    w = min(512, INC - c0)
                    bank = psB[:, (g % 4) * 512:(g % 4) * 512 + w]
                    for c in range(8):
                        S.add("pe", lambda e, c=c, c0=c0, w=w, bank=bank: e.matmul(bank, hT[:, c, :], wsb[:, c, c0:c0 + w], start=(c == 0), stop=(c == 7)),
                              ["hT", ("wsb", c)], [("psB", g % 4)])
                    S.add("act" if g % 2 else "dve",
                          (lambda e, c0=c0, w=w, bank=bank: e.copy(out=pr[:, c0:c0 + w], in_=bank)) if g % 2 else
                          (lambda e, c0=c0, w=w, bank=bank: e.tensor_copy(out=pr[:, c0:c0 + w], in_=bank)),
                          [("psB", g % 4)], [("pr", g)])
                PR = lambda a, b: [("pr", g) for g in range(a // 512, (b - 1) // 512 + 1)]
                def silu(dst, src, n, rk, wk, tmp, tk):
                    S.add("act", lambda e: e.activation(out=tmp, in_=src, func=AF.Exp, scale=-1.0), rk, [tk])
                    S.add("dve", lambda e: e.tensor_scalar_add(out=tmp, in0=tmp, scalar1=1.0), [tk], [tk])
                    S.add("dve", lambda e: e.reciprocal(out=tmp, in_=tmp), [tk], [tk])
                    S.add("dve", lambda e: e.tensor_tensor(out=dst, in0=src, in1=tmp, op=ALU.mult), rk + [tk], [wk])
                silu(tq[:], pr[:, 0:512], 512, PR(0, 512), "tq", tA[:], "tA")
                dma("sp", HQ[r0:r0 + 128, :], tq[:], ["tq"], [("HQ", t)])
                S.add("dve", lambda e, t=t: e.tensor_scalar(out=ropeb[:, 0:512], in0=pr[:, 1536:2048], scalar1=tmask[:, t:t + 1], scalar2=None, op0=ALU.mult),
                      PR(1536, 2048) + ["tmask"], ["rbA"])
                dma("sp", HV[r0:r0 + 128, :], ropeb[:, 0:512], ["rbA"], [("HV", t)])
                silu(gsb[:, 0:512], pr[:, 2048:2560], 512, PR(2048, 2560), "g0", tB[:], "tB")
                silu(gsb[:, 512:768], pr[:, 3072:3328], 256, PR(3072, 3328), "g1", tC[:, 0:256], "tC0")
                silu(gsb[:, 768:1024], pr[:, 4096:4352], 256, PR(4096, 4352), "g2", tC[:, 256:512], "tC1")
                dma("sp", GATE[r0:r0 + 128, :], gsb[:], ["g0", "g1", "g2"], [("GATE", t)])
                for d in range(2):
                    src = pr[:, 512 + d * 512:1024 + d * 512]
                    rk = PR(512 + d * 512, 1024 + d * 512)
                    fk = "tD"
                    S.add("act", lambda e, src=src: e.activation(out=tD[:], in_=src, func=AF.Exp, scale=-1.0), rk, [fk])
                    S.add("dve", lambda e: e.tensor_scalar_add(out=tD[:], in0=tD[:], scalar1=1.0), [fk], [fk])
                    S.add("dve", lambda e: e.reciprocal(out=tD[:], in_=tD[:]), [fk], [fk])
                    if l == 1:
                        S.add("dve", lambda e, d=d: e.tensor_tensor(out=tD[:], in0=tD[:], in1=oml1[:, d, :], op=ALU.mult), [fk, ("oml1", d)], [fk])
                        S.add("dve", lambda e, d=d: e.tensor_tensor(out=tD[:], in0=tD[:], in1=lb1[:, d, :], op=ALU.add), [fk, ("lb1", d)], [fk])
                    S.add("act", lambda e: e.activation(out=tA[:], in_=tD[:], func=AF.Ln), [fk], ["tA"])
                    dma("sp", HG[d, r0:r0 + 128, :], tA[:], ["tA"], [("HG", d, t)])
                    S.add("dve", lambda e: e.tensor_scalar(out=tB[:], in0=tD[:], scalar1=-1.0, scalar2=1.0, op0=ALU.mult, op1=ALU.add), [fk], ["tB"])
                    dma("sp", HK[d, r0:r0 + 128, :], tB[:], ["tB"], [("HK", d, t)])
                dma("sp", cB[:], cosB_d[r0:r0 + 128, :], [], ["cB"]); dma("sp", sB[:], sinB_d[r0:r0 + 128, :], [], ["sB"])
                dma("sp", cC[:], cosC_d[r0:r0 + 128, :], [], ["cC"]); dma("sp", sC[:], sinC_d[r0:r0 + 128, :], [], ["sC"])
                S.add("dve", lambda e: e.tensor_tensor(out=sq[:, 0:384], in0=pr[:, 2560:2944], in1=pr[:, 2560:2944], op=ALU.mult), PR(2560, 2944), ["sq"])
                S.add("dve", lambda e: e.reduce_sum(out=st[:, 1:7], in_=sq[:, 0:384].rearrange("p (h d) -> p h d", d=64), axis=AX.X), ["sq"], ["st"])
                rstd_from_ss(st[:, 1:7], 64, "st", "st")
                S.add("dve", lambda e: e.tensor_tensor(out=ropeq[:, 0:384].rearrange("p (h d) -> p h d", d=64), in0=pr[:, 2560:2944].rearrange("p (h d) -> p h d", d=64),
                                                       in1=st[:, 1:7].unsqueeze(2).broadcast_to([128, 6, 64]), op=ALU.mult), PR(2560, 2944) + ["st"], ["rqB"])
                S.add("dve", lambda e, l=l: e.tensor_tensor(out=ropeq[:, 0:256].rearrange("p (h d) -> p h d", d=64), in0=ropeq[:, 0:256].rearrange("p (h d) -> p h d", d=64),
                                                            in1=qnB[:, l, :].unsqueeze(1).broadcast_to([128, 4, 64]), op=ALU.mult), ["rqB", ("qnB", l)], ["rqB"])
                S.add("dve", lambda e, l=l: e.tensor_tensor(out=ropeq[:, 256:384].rearrange("p (h d) -> p h d", d=64), in0=ropeq[:, 256:384].rearrange("p (h d) -> p h d", d=64),
                                                            in1=knB[:, l, :].unsqueeze(1).broadcast_to([128, 2, 64]), op=ALU.mult), ["rqB", ("knB", l)], ["rqB"])
                S.add("act", lambda e: e.copy(out=ropeq[:, 384:896], in_=pr[:, 3328:3840]), PR(3328, 3840), ["rqC"])
                def rope(x_ap, n_heads_total, cos_t, sin_t, width, rk, wk):
                    nb = width // 32
                    xv = x_ap.rearrange("p (h b two j) -> p (h b) two j", b=nb, two=2, j=16)
                    G = n_heads_total * nb
                    t1 = sq[:, 0:G * 32].rearrange("p (g two j) -> p g two j", two=2, j=16)
                    cv = cos_t.rearrange("p (b two j) -> p b two j", two=2, j=16)
                    sv = sin_t.rearrange("p (b two j) -> p b two j", two=2, j=16)
                    x4 = x_ap.rearrange("p (h b two j) -> p h b two j", b=nb, two=2, j=16)
                    t4 = sq[:, 0:G * 32].rearrange("p (h b two j) -> p h b two j", b=nb, two=2, j=16)
                    for hh in range(2):
                        S.add("dve", lambda e, hh=hh: e.tensor_tensor(out=t4[:, :, :, hh, :], in0=x4[:, :, :, 1 - hh, :],
                                                                      in1=sv[:, :, hh, :].unsqueeze(1).broadcast_to([128, n_heads_total, nb, 16]), op=ALU.mult), rk, ["sq"])
                    S.add("dve", lambda e: e.tensor_tensor(out=x4.rearrange("p h b two j -> p h b (two j)"), in0=x4.rearrange("p h b two j -> p h b (two j)"),
                                                           in1=cos_t.rearrange("p (b w) -> p b w", w=32).unsqueeze(1).broadcast_to([128, n_heads_total, nb, 32]), op=ALU.mult),
                          rk + ["sq", "sq"], [wk])
                    S.add("dve", lambda e: e.tensor_tensor(out=x_ap, in0=x_ap, in1=sq[:, 0:G * 32], op=ALU.add), [wk, "sq", "sq"], [wk])
                rope(ropeq[:, 0:384], 6, cB[:], sB[:], 64, ["rqB", "rqB", "cB", "sB"], "rqB")
                rope(ropeq[:, 384:896], 16, cC[:], sC[:], 32, ["rqC", "cC", "sC"], "rqC")
                qv = ropeq[:, 0:256].rearrange("p (a b d) -> p a b d", a=2, b=2)
                qb_ = ropeb[:, 512:768].rearrange("p (b a d) -> p b a d", b=2, a=2)
                S.add("dve", lambda e: e.tensor_copy(out=qb_.rearrange("p b a d -> p a b d"), in_=qv), ["rqB"], ["rb0"])
                S.add("dve", lambda e: e.tensor_copy(out=ropeb[:, 768:896], in_=ropeq[:, 256:384]), ["rqB"], ["rb1"])
                S.add("act", lambda e: e.copy(out=ropeb[:, 896:1408 - 384], in_=ropeq[:, 384:896 - 384 + 384 - 384 + 128]), ["rqC"], ["rb2x"]) if False else None
                S.add("dve", lambda e: e.tensor_copy(out=ropeb[:, 0:512], in_=ropeq[:, 384:896]), ["rqC"], ["rbA"])
                srcs = [(512, "rb0"), (640, "rb0"), (0, "rbA"), (128, "rbA"), (768, "rb1"), (256, "rbA"), (384, "rbA")]
                psT2 = psA[:, 512:1024].bitcast(BF16)
                for i, (o, k) in enumerate(srcs):
                    S.add("pe", lambda e, i=i, o=o: e.transpose(psT2[:, i * 128:(i + 1) * 128], ropeb[:, o:o + 128], identb[:]), [k, "identb"], [("psA", 1)])
                S.add("act", lambda e: e.copy(out=qkT[:].rearrange("p i t -> p (i t)"), in_=psT2[:, 0:896]), [("psA", 1)], ["qkT"])
                dma("sp", QT[:, :, r0:r0 + 128].rearrange("i p t -> p i t"), qkT[:, 0:4, :], ["qkT"], [("QT", t)])
                dma("sp", KT[:, :, r0:r0 + 128].rearrange("i p t -> p i t"), qkT[:, 4:7, :], ["qkT"], [("KT", t)])
                for gi, c0 in enumerate((2944, 3840, 3968)):
                    S.add("dve", lambda e, gi=gi, c0=c0, t=t: e.tensor_scalar(out=vaug[:, gi, :].rearrange("p (h d) -> p h d", d=65)[:, :, 0:64],
                                                                            in0=pr[:, c0:c0 + 128].rearrange("p (h d) -> p h d", d=64), scalar1=tmask[:, t:t + 1], scalar2=None, op0=ALU.mult),
                          PR(c0, c0 + 128) + ["tmask"], [("vaug", gi)])
                    S.add("dve", lambda e, gi=gi, t=t: e.tensor_copy(out=vaug[:, gi, :].rearrange("p (h d) -> p h d", d=65)[:, :, 64:65],
                                                                    in_=tmask[:, t:t + 1].unsqueeze(1).broadcast_to([128, 2, 1])), ["tmask"], [("vaug1", gi)])
                dma("sp", VA[:, :, t, :].rearrange("g p c -> p g c"), vaug[:], [("vaug", g) for g in range(3)] + [("vaug1", g) for g in range(3)], [("VA", t)])


            for d in range(2):
                S.add("dve", lambda e: e.memset(S32[:], 0.0), [], ["S32"])
                S.add("dve", lambda e: e.memset(S16[:], 0.0), [], ["S16"])
                order = range(NT) if d == 0 else range(NT - 1, -1, -1)
                for t in order:
                    r0 = t * 128
                    dma("sp", hq[:], HQ[r0:r0 + 128, :], [("HQ", t)], ["hq"])
                    dma("sp", hv[:], HV[r0:r0 + 128, :], [("HV", t)], ["hv"])
                    dma("sp", hg[:], HG[d, r0:r0 + 128, :], [("HG", d, t)], ["hg"])
                    dma("sp", hk[:], HK[d, r0:r0 + 128, :], [("HK", d, t)], ["hk"])
                    for i in range(3):
                        S.add("pe", lambda e, i=i, d=d: e.matmul(psA[:, i * 512:(i + 1) * 512], hm[:, 3 * d + i, :], hg[:], start=True, stop=True), ["hm", "hg"], [("psA", i)])
                    S.add("act", lambda e: e.activation(out=tC[:], in_=psA[:, 0:512], func=AF.Exp), [("psA", 0)], ["tC0", "tC1"])
                    S.add("act", lambda e: e.activation(out=tA[:], in_=psA[:, 512:1024], func=AF.Exp), [("psA", 1)], ["tA"])
                    S.add("act", lambda e: e.activation(out=tB[:], in_=psA[:, 512:1024], func=AF.Exp, scale=-1.0), [("psA", 1)], ["tB"])
                    S.add("act", lambda e: e.activation(out=tD[:], in_=psA[:, 1024:1536], func=AF.Exp), [("psA", 2)], ["tD"])
                    S.add("dve", lambda e: e.tensor_tensor(out=qmb[:], in0=hq[:], in1=tA[:], op=ALU.mult), ["hq", "tA"], ["qmb"])
                    S.add("dve", lambda e: e.tensor_tensor(out=kmb[:], in0=hk[:], in1=tB[:], op=ALU.mult), ["hk", "tB"], ["kmb"])
                    S.add("dve", lambda e: e.tensor_tensor(out=qbb[:], in0=hq[:], in1=tC[:], op=ALU.mult), ["hq", "tC0", "tC1"], ["qbb"])
                    S.add("dve", lambda e: e.tensor_tensor(out=tD[:], in0=hk[:], in1=tD[:], op=ALU.mult), ["hk", "tD"], ["tD"])
                    for c in range(4):
                        S.add("dve", lambda e, c=c: e.tensor_scalar(out=kdm[:, c, :], in0=tD[:], scalar1=sel[:, c:c + 1], scalar2=None, op0=ALU.mult), ["tD", "sel"], [("kdm", c)])
                    for h in range(4):
                        S.add("pe", lambda e, h=h: e.matmul(psA[:, 1536 + h * 4:1536 + h * 4 + 4], hg[:, h * 128:(h + 1) * 128], sel[:], start=True, stop=True), ["hg", "sel"], [("psA", 3)])
                    S.add("act", lambda e: e.activation(out=dec[:], in_=psA[:, 1536:1552], func=AF.Exp), [("psA", 3)], ["dec"])
                    pb0 = psB[:, 0:512].bitcast(BF16)
                    pb1 = psB[:, 512:1024].bitcast(BF16)
                    for h in range(4):
                        S.add("pe", lambda e, h=h: e.transpose(pb0[:, h * 128:(h + 1) * 128], qmb[:, h * 128:(h + 1) * 128], identb[:]), ["qmb", "identb"], [("psB", 0)])
                        S.add("pe", lambda e, h=h: e.transpose(pb0[:, 512 + h * 128:512 + (h + 1) * 128], kmb[:, h * 128:(h + 1) * 128], identb[:]), ["kmb", "identb"], [("psB", 0)])
                        S.add("pe", lambda e, h=h: e.transpose(pb1[:, h * 128:(h + 1) * 128], qbb[:, h * 128:(h + 1) * 128], identb[:]), ["qbb", "identb"], [("psB", 1)])
                    S.add("dve", lambda e: e.tensor_copy(out=qkmT[:], in_=pb0[:, 0:1024]), [("psB", 0)], ["qkmT"])
                    S.add("act", lambda e: e.copy(out=qbT[:], in_=pb1[:, 0:512]), [("psB", 1)], ["qbT"])
                    for h in range(4):
                        S.add("pe", lambda e, h=h: e.matmul(psB[:, 1024 + h * 128:1024 + (h + 1) * 128], qkmT[:, 512 + h * 128:512 + (h + 1) * 128], qkmT[:, h * 128:(h + 1) * 128], start=True, stop=True),
                              ["qkmT"], [("psB", 2)])
                    S.add("dve", lambda e, d=d: e.tensor_tensor(out=ATb[:].rearrange("p (h t) -> p h t", h=4), in0=psB[:, 1024:1536].rearrange("p (h t) -> p h t", h=4),
                                                                in1=am[:, d, :].unsqueeze(1).broadcast_to([128, 4, 128]), op=ALU.mult), [("psB", 2), "am"], ["ATb"])
                    corder = range(4) if d == 0 else range(3, -1, -1)
                    for ci, c in enumerate(corder):
                        for h in range(4):
                            o_ap = psB[:, 1536 + h * 128 + c * 32:1536 + h * 128 + (c + 1) * 32]
                            S.add("pe", lambda e, h=h, c=c, o_ap=o_ap: e.matmul(o_ap, hv[:, h * 128:(h + 1) * 128], ATb[:, h * 128 + c * 32:h * 128 + (c + 1) * 32], start=True, stop=False),
                                  ["hv", "ATb"], [("psB", 3)])
                            S.add("pe", lambda e, h=h, c=c, o_ap=o_ap: e.matmul(o_ap, S16[:, h * 128:(h + 1) * 128], qbT[:, h * 128 + c * 32:h * 128 + (c + 1) * 32], start=False, stop=True),
                                  ["S16", "qbT"], [("psB", 3)])
                        bk = 0 if ci % 2 == 0 else 1
                        for h in range(4):
                            S.add("pe", lambda e, h=h, c=c, bk=bk: e.matmul(psA[:, bk * 512 + h * 128:bk * 512 + (h + 1) * 128], kdm[:, c, h * 128:(h + 1) * 128], hv[:, h * 128:(h + 1) * 128], start=True, stop=True),
                                  [("kdm", c), "hv"], [("psA", bk)])
                        for h in range(4):
                            S.add("dve", lambda e, h=h, c=c, bk=bk: e.scalar_tensor_tensor(out=S32[:, h * 128:(h + 1) * 128], in0=S32[:, h * 128:(h + 1) * 128], scalar=dec[:, h * 4 + c:h * 4 + c + 1],
                                                                                         in1=psA[:, bk * 512 + h * 128:bk * 512 + (h + 1) * 128], op0=ALU.mult, op1=ALU.add),
                                  ["S32", "dec", ("psA", bk)], ["S32"])
                        S.add("act", lambda e: e.copy(out=S16[:], in_=S32[:]), ["S32"], ["S16"])
                    S.add("dve", lambda e: e.tensor_copy(out=oTs[:], in_=psB[:, 1536:2048]), [("psB", 3)], ["oTs"])
                    for h in range(4):
                        S.add("pe", lambda e, h=h: e.transpose(psA[:, 1024 + h * 128:1024 + (h + 1) * 128], oTs[:, h * 128:(h + 1) * 128], ident[:]), ["oTs", "ident"], [("psA", 2)])
                    if d == 0:
                        S.add("act", lambda e: e.copy(out=otok[:], in_=psA[:, 1024:1536]), [("psA", 2)], ["otok"])
                        dma("sp", OH[r0:r0 + 128, :], otok[:], ["otok"], [("OH", t)])
                    else:
                        dma("sp", otok[:], OH[r0:r0 + 128, :], [("OH", t)], ["otok"])
                        dma("sp", gsb[:, 0:512], GATE[r0:r0 + 128, 0:512], [("GATE", t)], ["g0"])
                        S.add("dve", lambda e: e.tensor_tensor(out=otok[:], in0=otok[:], in1=psA[:, 1024:1536], op=ALU.add), ["otok", ("psA", 2)], ["otok"])
                        S.add("dve", lambda e: e.tensor_tensor(out=sq[:, 0:512], in0=otok[:], in1=otok[:], op=ALU.mult), ["otok"], ["sq"])
                        S.add("dve", lambda e: e.reduce_sum(out=st[:, 0:4], in_=sq[:, 0:512].rearrange("p (h d) -> p h d", h=4), axis=AX.X), ["sq"], ["st"])
                        rstd_from_ss(st[:, 0:4], 128, "st", "st")
                        S.add("dve", lambda e: e.tensor_tensor(out=otok[:].rearrange("p (h d) -> p h d", h=4), in0=otok[:].rearrange("p (h d) -> p h d", h=4),
                                                               in1=st[:, 0:4].unsqueeze(2).broadcast_to([128, 4, 128]), op=ALU.mult), ["otok", "st"], ["otok"])
                        S.add("dve", lambda e, l=l: e.tensor_tensor(out=otok[:], in0=otok[:], in1=hnB[:, l, :], op=ALU.mult), ["otok", ("hnB", l)], ["otok"])
                        S.add("dve", lambda e: e.tensor_tensor(out=mixa[:], in0=otok[:], in1=gsb[:, 0:512], op=ALU.mult), ["otok", "g0"], ["mixa"])
                        dma("sp", MIX[r0:r0 + 128, 0:512], mixa[:], ["mixa"], [("MIXA", t)])

            QJ = min(1024, T)
            NQG = QJ // 512 if QJ >= 512 else 1
            QW = min(512, QJ)
            for gi in range(3):
                dma("sp", KTs, KT[gi, :, :], [("KT", t) for t in range(NT)], ["KTs"] + [("wsb", c) for c in range(8)])
                dma("sp", VAs, VA[gi, :, :, :], [("VA", t) for t in range(NT)], ["VAs"] + [("wsb", c) for c in range(8)])
                if gi == 0:
                    units = [(b, a * 64, 64, a, a * 2 + b) for b in range(2) for a in range(2)]
                else:
                    units = [(2 + gi - 1, hh * 64 + m * 32, 32, hh, 4 + (2 * (gi - 1) + hh) * 2 + m) for hh in range(2) for m in range(2)]
                for (qt, base, dk, vh, slot) in units:
                    scale = 1.0 / math.sqrt(dk)
                    for j in range(T // QJ):
                        q0 = j * QJ
                        dma("sp", qs[:], QT[qt, :, q0:q0 + QJ], [("QT", t) for t in range(NT)], ["qs"])
                        for kt in range(NT):
                            sb_ = kt % 2
                            for qg in range(NQG):
                                S.add("pe", lambda e, kt=kt, qg=qg, sb_=sb_, base=base, dk=dk: e.matmul(
                                    psA[:, (sb_ * 2 + qg) * 512:(sb_ * 2 + qg) * 512 + QW], KTs[base:base + dk, kt * 128:(kt + 1) * 128], qs[base:base + dk, qg * QW:(qg + 1) * QW],
                                    start=True, stop=True, tile_position=((96, 0) if base == 96 else None)), ["KTs", "qs"], [("psA", 2 * sb_), ("psA", 2 * sb_ + 1)])
                            S.add("act", lambda e, sb_=sb_, scale=scale: e.activation(out=pT[:, sb_, 0:QJ], in_=psA[:, sb_ * 1024:sb_ * 1024 + QJ], func=AF.Exp, scale=scale), [("psA", 2 * sb_), ("psA", 2 * sb_ + 1)], [("pT", sb_)])
                            for qg in range(NQG):
                                S.add("pe", lambda e, kt=kt, qg=qg, sb_=sb_, vh=vh: e.matmul(psB[0:65, qg * 512:qg * 512 + QW], VAs[:, kt, vh * 65:(vh + 1) * 65], pT[:, sb_, qg * QW:(qg + 1) * QW],
                                                                                          start=(kt == 0), stop=(kt == NT - 1)), ["VAs", ("pT", sb_)], [("psB", 0), ("psB", 1)])
                        S.add("dve", lambda e: e.tensor_copy(out=oas[:, 0:QJ], in_=psB[0:65, 0:QJ]), [("psB", 0), ("psB", 1)], ["oas"])
                        for jb in range(QJ // 128):
                            S.add("pe", lambda e, jb=jb: e.transpose(psB[:, 1024 + jb * 128:1024 + jb * 128 + 65], oas[:, jb * 128:(jb + 1) * 128], ident[0:65, 0:65]), ["oas", "ident"], [("psB", 2), ("psB", 3)])
                        nb_ = QJ // 128
                        S.add("dve", lambda e, nb_=nb_: e.reciprocal(out=rz[:, 0:nb_], in_=psB[:, 1024:1024 + nb_ * 128].rearrange("p (j c) -> p j c", c=128)[:, :, 64]), [("psB", 2), ("psB", 3)], ["rz"])
                        S.add("dve", lambda e, nb_=nb_: e.tensor_tensor(out=aos[:, 0:nb_, :], in0=psB[:, 1024:1024 + nb_ * 128].rearrange("p (j c) -> p j c", c=128)[:, :, 0:64],
                                                                        in1=rz[:, 0:nb_].unsqueeze(2).broadcast_to([128, nb_, 64]), op=ALU.mult), [("psB", 2), ("psB", 3), "rz"], ["aos"])
                        dma("sp", AO[q0:q0 + QJ, slot, :].rearrange("(j p) d -> p j d", p=128), aos[:, 0:nb_, :], ["aos"], [("AO", slot, j)])

            lam_init = 0.8 - 0.6 * math.exp(-0.3 * l)
            for t in range(NT):
                r0 = t * 128
                dma("sp", mixb[:, 0:512], MIX[r0:r0 + 128, 0:512], [("MIXA", t)], ["mixb0"])
                dma("sp", aot[:], AO[r0:r0 + 128, :, :].rearrange("p s d -> p (s d)"), [("AO", s_, j_) for s_ in range(12) for j_ in range(T // QJ)], ["aot"])
                dma("sp", gsb[:], GATE[r0:r0 + 128, :], [("GATE", t)], ["g0", "g1", "g2"])
                dma("sp", xt[:], xin[r0:r0 + 128, :], ([("x1", t)] if l == 1 else []), ["xt"])
                S.add("dve", lambda e: e.tensor_tensor(out=mixb[:, 512:768], in0=aot[:, 0:256], in1=gsb[:, 512:768], op=ALU.mult), ["aot", "g1"], ["mixb1"])
                av = aot[:, 256:768].rearrange("p (h m d) -> p h m d", h=4, m=2)
                S.add("dve", lambda e, l=l, av=av: e.scalar_tensor_tensor(out=tA[:, 0:256].rearrange("p (h d) -> p h d", h=4), in0=av[:, :, 1, :], scalar=lamv[:, l, 3:4], in1=av[:, :, 0, :],
                                                                        op0=ALU.mult, op1=ALU.add), ["aot", ("nlam", l)], ["tA"])
                S.add("dve", lambda e: e.tensor_tensor(out=sq[:, 0:256], in0=tA[:, 0:256], in1=tA[:, 0:256], op=ALU.mult), ["tA"], ["sq"])
                S.add("dve", lambda e: e.reduce_sum(out=st[:, 0:4], in_=sq[:, 0:256].rearrange("p (h d) -> p h d", h=4), axis=AX.X), ["sq"], ["st"])
                rstd_from_ss(st[:, 0:4], 64, "st", "st")
                S.add("dve", lambda e: e.tensor_tensor(out=tA[:, 0:256].rearrange("p (h d) -> p h d", h=4), in0=tA[:, 0:256].rearrange("p (h d) -> p h d", h=4),
                                                       in1=st[:, 0:4].unsqueeze(2).broadcast_to([128, 4, 64]), op=ALU.mult), ["tA", "st"], ["tA"])
                S.add("dve", lambda e, l=l: e.tensor_tensor(out=tA[:, 0:256].rearrange("p (h d) -> p h d", h=4), in0=tA[:, 0:256].rearrange("p (h d) -> p h d", h=4),
                                                            in1=dnB[:, l, :].unsqueeze(1).broadcast_to([128, 4, 64]), op=ALU.mult), ["tA", ("dnB", l)], ["tA"])
                S.add("dve", lambda e, li=lam_init: e.scalar_tensor_tensor(out=mixb[:, 768:1024], in0=tA[:, 0:256], scalar=1.0 - li, in1=gsb[:, 768:1024], op0=ALU.mult, op1=ALU.mult), ["tA", "g2"], ["mixb2"])
                pm = psA[:, 0:512].bitcast(BF16)
                for c in range(8):
                    S.add("pe", lambda e, c=c: e.transpose(pm[:, c * 128:(c + 1) * 128], mixb[:, c * 128:(c + 1) * 128], identb[:]), ["mixb0", "mixb1", "mixb2", "identb"], [("psA", 0)])
                S.add("dve", lambda e: e.tensor_copy(out=hT[:].rearrange("p c t -> p (c t)"), in_=pm[:, 0:1024]), [("psA", 0)], ["hT"])
                for n in range(2):
                    for c in range(8):
                        S.add("pe", lambda e, c=c, n=n: e.matmul(psB[:, n * 512:(n + 1) * 512], hT[:, c, :], wosb[:, c, n * 512:(n + 1) * 512], start=(c == 0), stop=(c == 7)),
                              ["hT", ("wosb", c)], [("psB", n)])
                S.add("act", lambda e: e.activation(out=sq[:], in_=psB[:, 0:1024], func=AF.Square, accum_out=st[:, 4:5]), [("psB", 0), ("psB", 1)], ["sq", "st"])
                rstd_from_ss(st[:, 4:5], D, "st", "st")
                S.add("dve", lambda e, l=l: e.scalar_tensor_tensor(out=pr[:, 0:1024], in0=psB[:, 0:1024], scalar=st[:, 4:5], in1=postB1[:, 0, :], op0=ALU.mult, op1=ALU.mult),
                      [("psB", 0), ("psB", 1), "st", "postB"], [("pr", 0), ("pr", 1)])
                S.add("dve", lambda e: e.tensor_tensor(out=pr[:, 0:1024], in0=pr[:, 0:1024], in1=xt[:], op=ALU.add), [("pr", 0), ("pr", 1), "xt"], [("pr", 0), ("pr", 1)])
                dma("sp", xout[r0:r0 + 128, :], pr[:, 0:1024], [("pr", 0), ("pr", 1)], [("x1", t) if l == 0 else ("yout", t)], is_out=(l == 1))

        S.emit(nc, stack)
    return nc


def kernel(**inputs):
    T = 16384
    NT = T // 128
    xp = np.asarray(inputs["x_prompt"], np.float32)
    xs = np.asarray(inputs["x_sample"], np.float32)
    seqs = [xp[0]] + [xs[b] for b in range(xs.shape[0])]
    while len(seqs) < NCORES:
        seqs.append(xs[0])
    consts = make_consts(T)
    shared = {k: np.ascontiguousarray(np.asarray(inputs[k], np.float32)) for k in
              ("w_in", "w_out", "pre_norm_w", "post_norm_w", "hgrn_lb", "hgrn_norm_w", "gqa_q_norm_w",
               "gqa_k_norm_w", "diff_lambda", "diff_norm_w")}
    in_maps = []
    for c in range(NCORES):
        sq_ = seqs[c]
        L = sq_.shape[0]
        xpad = np.zeros((T, D), np.float32)
        xpad[:L] = sq_
        idx = np.arange(128)[:, None] + 128 * np.arange(NT)[None, :]
        tmask = (idx < L).astype(np.float32)
        m = dict(x=xpad, tmask=tmask)
        m.update(shared)
        m.update(consts)
        in_maps.append(m)
    nc = build(T)
    res = run_bass_kernel_spmd(nc, in_maps, core_ids=list(range(NCORES)))
    r = res.results
    y_prompt = np.asarray(r[0]["y"], np.float32)[None, :xp.shape[1], :]
    y_sample = np.stack([np.asarray(r[1 + b]["y"], np.float32)[:xs.shape[1]] for b in range(xs.shape[0])], 0)
    return (np.ascontiguousarray(y_prompt), np.ascontiguousarray(y_sample))
```

```python
import math
import numpy as np
import ml_dtypes
import concourse.bass as bass
import concourse.mybir as mybir
from concourse.bass_utils import run_bass_kernel_spmd

F32 = mybir.dt.float32
BF16 = mybir.dt.bfloat16
ALU = mybir.AluOpType
AF = mybir.ActivationFunctionType
AX = mybir.AxisListType

D = 1024
DEPTH = 2
INC = 4352
EPS = 1e-6
NCORES = 8
CH = 32
DEBUG = False


class Op:
    __slots__ = ("idx", "eng", "fn", "deps", "dma", "lidx", "marked", "count", "epoch", "sem", "semval")

    def __init__(self, idx, eng, fn, dma):
        self.idx = idx; self.eng = eng; self.fn = fn; self.dma = dma
        self.deps = set(); self.marked = False; self.count = 0; self.epoch = 0
        self.sem = None; self.semval = 0; self.lidx = 0


class Sched:
    ENGS = ("pe", "act", "dve", "pool", "sp")
    NDS = 20
    EPOCH = 20000

    def __init__(self):
        self.ops = []
        self.lw = {}
        self.rd = {}
        self.per = {e: [] for e in self.ENGS}

    def add(self, eng, fn, reads=(), writes=(), dma=False):
        op = Op(len(self.ops), eng, fn, dma)
        for r in reads:
            if r in self.lw:
                op.deps.add(self.lw[r])
        for w in writes:
            if w in self.lw:
                op.deps.add(self.lw[w])
            for x in self.rd.get(w, ()):
                op.deps.add(x)
        for r in reads:
            self.rd.setdefault(r, []).append(op.idx)
        for w in writes:
            self.lw[w] = op.idx
            self.rd[w] = []
        op.deps.discard(op.idx)
        op.lidx = len(self.per[eng])
        self.per[eng].append(op)
        self.ops.append(op)
        return op

    def emit(self, nc, stack):
        ops = self.ops
        for op in ops:
            for d in op.deps:
                p = ops[d]
                if p.dma:
                    continue
                if p.eng == op.eng:
                    if p.eng == "pe":
                        continue
                    if op.dma or p.lidx >= op.lidx - 3:
                        p.marked = True
                else:
                    p.marked = True
        nep = {}
        for e in self.ENGS:
            c = 0
            for op in self.per[e]:
                if op.dma:
                    continue
                if op.marked:
                    c += 1
                    op.epoch = (c - 1) // self.EPOCH
                    op.count = (c - 1) % self.EPOCH + 1
            nep[e] = max(1, (c + self.EPOCH - 1) // self.EPOCH)
        csem = {e: [stack.enter_context(nc.semaphore(f"c_{e}_{i}")) for i in range(nep[e])] for e in self.ENGS}
        dsem = {e: [stack.enter_context(nc.semaphore(f"d_{e}_{i}")) for i in range(self.NDS)] for e in ("sp", "pool", "act")}
        dcount = {e: [0] * self.NDS for e in dsem}
        dprev = {e: [None] * self.NDS for e in dsem}
        for e in dsem:
            j = 0
            for op in self.per[e]:
                if op.dma:
                    k = j % self.NDS
                    dcount[e][k] += 1
                    op.sem = dsem[e][k]
                    op.semval = 16 * dcount[e][k]
                    op.count = dprev[e][k]
                    dprev[e][k] = op
                    j += 1
        block = stack.enter_context(nc.Block())
        final_waits = [op for op in ops if op.dma and op.fn is not None and getattr(op.fn, "_is_out", False)]

        def run_engine(e, eng):
            wm = {}

            def wait(sem, val):
                key = id(sem)
                if wm.get(key, 0) >= val:
                    return
                eng.wait_ge(sem, val)
                wm[key] = val

            for op in self.per[e]:
                for d in sorted(op.deps):
                    p = ops[d]
                    if p.dma:
                        wait(p.sem, p.semval)
                    elif p.eng == e:
                        if e != "pe" and p.marked and (op.dma or p.lidx >= op.lidx - 3):
                            wait(csem[e][p.epoch], p.count)
                    else:
                        wait(csem[p.eng][p.epoch], p.count)
                if op.dma:
                    if op.count is not None:
                        wait(op.count.sem, op.count.semval)
                    ins = op.fn(eng)
                    ins.then_inc(op.sem, 16)
                else:
                    ins = op.fn(eng)
                    if op.marked:
                        ins.then_inc(csem[e][op.epoch], 1)
            if e == "sp":
                for op in final_waits:
                    wait(op.sem, op.semval)

        @block.sync
        def _(eng):
            run_engine("sp", eng)

        @block.gpsimd
        def _(eng):
            run_engine("pool", eng)

        @block.tensor
        def _(eng):
            run_engine("pe", eng)

        @block.scalar
        def _(eng):
            run_engine("act", eng)

        @block.vector
        def _(eng):
            run_engine("dve", eng)


def _rope_tables(pos, dim):
    inv = np.power(10000.0, -np.arange(0, dim, 2, dtype=np.float32) / dim).astype(np.float32)
    ang = pos[:, None].astype(np.float32) * inv[None, :]
    ang = np.concatenate([ang, ang], axis=-1)
    c = np.cos(ang).astype(np.float32)
    s = np.sin(ang).astype(np.float32)
    half = dim // 2
    s[:, :half] *= -1.0
    return c, s


def make_consts(T):
    pos = np.arange(T, dtype=np.float32)
    row = np.floor(pos / 64.0).astype(np.float32)
    col = (pos - 64.0 * row).astype(np.float32)
    cr, sr = _rope_tables(row, 32)
    cc, sc = _rope_tables(col, 32)
    cosB = np.concatenate([cr, cc], -1)
    sinB = np.concatenate([sr, sc], -1)
    cosC, sinC = _rope_tables(pos, 32)
    p = np.arange(128)
    ch = p // CH
    ps = p % CH
    same = (ch[:, None] == ch[None, :]).astype(np.float32)
    s_idx = p[:, None]
    t_idx = p[None, :]
    mats = {}
    for d in (0, 1):
        if d == 0:
            tri = same * (s_idx <= t_idx)
            midm = same * (ps[:, None] <= 16)
        else:
            tri = same * (s_idx >= t_idx)
            midm = same * (ps[:, None] >= 15)
        mats[d] = dict(tri=tri.astype(np.float32), d1=(tri - midm).astype(np.float32),
                       d3=(same - tri).astype(np.float32), am=tri.astype(np.float32))
    sel = (ch[:, None] == np.arange(4)[None, :]).astype(np.float32)
    hm = np.stack([mats[0]["tri"], mats[0]["d1"], mats[0]["d3"], mats[1]["tri"], mats[1]["d1"], mats[1]["d3"]], 0)
    am = np.stack([mats[0]["am"], mats[1]["am"]], 0)
    return dict(cosB=cosB, sinB=sinB, cosC=cosC, sinC=sinC, hm=hm.astype(np.float32), am=am.astype(np.float32),
                sel=sel, ident=np.eye(128, dtype=np.float32))


def build(T):
    NT = T // 128
    nc = bass.Bass("TRN2", target_bir_lowering=False)
    S = Sched()

    def din(name, shape, dt=F32):
        return nc.dram_tensor(name, list(shape), dt, kind="ExternalInput").ap()

    x_in = din("x", [T, D])
    w_in_d = din("w_in", [DEPTH, D, INC])
    w_out_d = din("w_out", [DEPTH, D, D])
    pre_w = din("pre_norm_w", [DEPTH, D])
    post_w = din("post_norm_w", [DEPTH, D])
    hlb = din("hgrn_lb", [DEPTH, 2, 512])
    hnw = din("hgrn_norm_w", [DEPTH, 512])
    qnw = din("gqa_q_norm_w", [DEPTH, 64])
    knw = din("gqa_k_norm_w", [DEPTH, 64])
    dlam = din("diff_lambda", [DEPTH, 4, 32])
    dnw = din("diff_norm_w", [DEPTH, 64])
    cosB_d = din("cosB", [T, 64]); sinB_d = din("sinB", [T, 64])
    cosC_d = din("cosC", [T, 32]); sinC_d = din("sinC", [T, 32])
    hm_d = din("hm", [6, 128, 128]); am_d = din("am", [2, 128, 128])
    sel_d = din("sel", [128, 4]); ident_d = din("ident", [128, 128])
    tmask_d = din("tmask", [128, NT])
    y_out = nc.dram_tensor("y", [T, D], F32, kind="ExternalOutput").ap()

    def dscr(name, shape, dt):
        if DEBUG:
            return nc.dram_tensor(name, list(shape), dt, kind="ExternalOutput").ap()
        return nc.dram_tensor(name, list(shape), dt).ap()

    x1 = dscr("x1", [T, D], F32)
    QT = dscr("QT", [4, 128, T], BF16)
    KT = dscr("KT", [3, 128, T], BF16)
    VA = dscr("VA", [3, 128, NT, 130], BF16)
    HQ = dscr("HQ", [T, 512], BF16)
    HV = dscr("HV", [T, 512], BF16)
    HG = dscr("HG", [2, T, 512], F32)
    HK = dscr("HK", [2, T, 512], F32)
    GATE = dscr("GATE", [T, 1024], F32)
    OH = dscr("OH", [T, 512], F32)
    AO = dscr("AO", [T, 12, 64], F32)
    MIX = dscr("MIX", [T, 1024], BF16)

    import contextlib
    stack = contextlib.ExitStack()
    with stack:
        def sb(name, shape, dt=F32):
            return stack.enter_context(nc.sbuf_tensor("s_" + name, list(shape), dt))

        def pst(name, shape, dt=F32):
            return stack.enter_context(nc.psum_tensor("p_" + name, list(shape), dt))

        ident = sb("ident", [128, 128]); identb = sb("identb", [128, 128], BF16)
        hm = sb("hm", [128, 6, 128]); am = sb("am", [128, 2, 128])
        sel = sb("sel", [128, 4]); tmask = sb("tmask", [128, NT])
        preB1 = sb("preB", [128, 1, D]); postB1 = sb("postB", [128, 1, D])
        hnB = sb("hnB", [128, DEPTH, 512]); qnB = sb("qnB", [128, DEPTH, 64]); knB = sb("knB", [128, DEPTH, 64])
        dnB = sb("dnB", [128, DEPTH, 64]); lamB = sb("lamB", [128, DEPTH, 128])
        lb1 = sb("lb1", [128, 2, 512]); oml1 = sb("oml1", [128, 2, 512])
        lamv = sb("lamv", [128, DEPTH, 4]); lamt = sb("lamt", [128, DEPTH, 128])
        eps_t = sb("eps_t", [128, 1]); one_t = sb("one_t", [128, 1])

        def bc(ap2d, n):
            return ap2d.partition_broadcast(128)

        def dma(eng, out, in_, reads, writes, is_out=False):
            def fn(e, out=out, in_=in_):
                return e.dma_start(out=out, in_=in_)
            fn._is_out = is_out
            return S.add(eng, fn, reads, writes, dma=True)

        dma("sp", ident[:], ident_d[:, :], [], ["ident"])
        dma("sp", hm[:], hm_d.rearrange("m p q -> p m q"), [], ["hm"])
        dma("sp", am[:], am_d.rearrange("m p q -> p m q"), [], ["am"])
        dma("sp", sel[:], sel_d[:, :], [], ["sel"])
        dma("sp", tmask[:], tmask_d[:, :], [], ["tmask"])
        for l in range(DEPTH):
            dma("sp", hnB[:, l, :], hnw[l:l + 1, :].partition_broadcast(128), [], [("hnB", l)])
            dma("sp", qnB[:, l, :], qnw[l:l + 1, :].partition_broadcast(128), [], [("qnB", l)])
            dma("sp", knB[:, l, :], knw[l:l + 1, :].partition_broadcast(128), [], [("knB", l)])
            dma("sp", dnB[:, l, :], dnw[l:l + 1, :].partition_broadcast(128), [], [("dnB", l)])
            dma("sp", lamB[:, l, :], dlam[l:l + 1, :, :].rearrange("o a b -> o (a b)").partition_broadcast(128), [], [("lamB", l)])
        S.add("dve", lambda e: e.memset(eps_t[:], EPS), [], ["eps"])
        S.add("dve", lambda e: e.memset(one_t[:], 1.0), [], ["one"])
        S.add("dve", lambda e: e.tensor_copy(out=identb[:], in_=ident[:]), ["ident"], ["identb"])
        for l in range(DEPTH):
            lam_init = 0.8 - 0.6 * math.exp(-0.3 * l)
            S.add("dve", lambda e, l=l: e.tensor_tensor(out=lamt[:, l, 0:32], in0=lamB[:, l, 0:32], in1=lamB[:, l, 32:64], op=ALU.mult), [("lamB", l)], [("lamt", l, 0)])
            S.add("dve", lambda e, l=l: e.tensor_tensor(out=lamt[:, l, 32:64], in0=lamB[:, l, 64:96], in1=lamB[:, l, 96:128], op=ALU.mult), [("lamB", l)], [("lamt", l, 1)])
            S.add("dve", lambda e, l=l: e.reduce_sum(out=lamv[:, l, 0:2], in_=lamt[:, l, 0:64].rearrange("p (a b) -> p a b", a=2), axis=AX.X),
                  [("lamt", l, 0), ("lamt", l, 1)], [("lamv", l)])
            S.add("act", lambda e, l=l: e.activation(out=lamv[:, l, 0:2], in_=lamv[:, l, 0:2], func=AF.Exp), [("lamv", l)], [("lamv", l)])
            S.add("dve", lambda e, l=l: e.tensor_sub(out=lamv[:, l, 2:3], in0=lamv[:, l, 0:1], in1=lamv[:, l, 1:2]), [("lamv", l)], [("lamv2", l)])
            S.add("dve", lambda e, l=l, li=lam_init: e.tensor_scalar(out=lamv[:, l, 3:4], in0=lamv[:, l, 2:3], scalar1=li, scalar2=-1.0, op0=ALU.add, op1=ALU.mult),
                  [("lamv2", l)], [("nlam", l)])

        wsb = sb("wsb", [128, 8, INC], BF16)
        wosb = sb("wosb", [128, 8, D], BF16)
        xt = sb("xt", [128, D]); sq = sb("sq", [128, D]); hb = sb("hb", [128, D], BF16)
        hT = sb("hT", [128, 8, 128], BF16)
        st = sb("st", [128, 8])
        pr = sb("pr", [128, INC])
        lbB = pr[:, 0:2048].rearrange("p (a n) -> p a n", a=4)
        for l_ in range(DEPTH):
            for d in range(2):
                dma("sp", lbB[:, l_ * 2 + d, :], hlb[l_, d:d + 1, :].partition_broadcast(128), [], [("lbB", l_, d), ("pr", l_ * 2 + d)])
        for d in range(2):
            S.add("dve", lambda e, d=d: e.tensor_sub(out=lb1[:, d, :], in0=lbB[:, d, :], in1=lbB[:, 2 + d, :]),
                  [("lbB", 0, d), ("lbB", 1, d), ("pr", d), ("pr", 2 + d)], [("lb1", d)])
            S.add("act", lambda e, d=d: e.activation(out=lb1[:, d, :], in_=lb1[:, d, :], func=AF.Exp), [("lb1", d)], [("lb1", d)])
            S.add("dve", lambda e, d=d: e.tensor_scalar_add(out=lb1[:, d, :], in0=lb1[:, d, :], scalar1=1.0), [("lb1", d)], [("lb1", d)])
            S.add("dve", lambda e, d=d: e.reciprocal(out=lb1[:, d, :], in_=lb1[:, d, :]), [("lb1", d)], [("lb1", d)])
            S.add("dve", lambda e, d=d: e.tensor_scalar(out=oml1[:, d, :], in0=lb1[:, d, :], scalar1=-1.0, scalar2=1.0, op0=ALU.mult, op1=ALU.add),
                  [("lb1", d)], [("oml1", d)])
        tA = sb("tA", [128, 512]); tB = sb("tB", [128, 512]); tC = sb("tC", [128, 512]); tD = sb("tD", [128, 512])
        gsb = sb("gsb", [128, 1024])
        ropeq = sb("ropeq", [128, 1024]); ropeb = sb("ropeb", [128, 1024], BF16)
        cB = sb("cB", [128, 64]); sB = sb("sB", [128, 64]); cC = sb("cC", [128, 32]); sC = sb("sC", [128, 32])
        tq = sb("tq", [128, 512], BF16)
        vaug = sb("vaug", [128, 3, 130], BF16)
        qkT = sb("qkT", [128, 7, 128], BF16)
        hq = sb("hq", [128, 512], BF16); hv = sb("hv", [128, 512], BF16); hg = sb("hg", [128, 512]); hk = sb("hk", [128, 512])
        qmb = sb("qmb", [128, 512], BF16); kmb = sb("kmb", [128, 512], BF16); qbb = sb("qbb", [128, 512], BF16)
        kdm = sb("kdm", [128, 4, 512], BF16); dec = sb("dec", [128, 16])
        qkmT = sb("qkmT", [128, 1024], BF16); qbT = sb("qbT", [128, 512], BF16); ATb = sb("ATb", [128, 512], BF16)
        S32 = sb("S32", [128, 512]); S16 = sb("S16", [128, 512], BF16)
        oTs = sb("oTs", [128, 512]); otok = sb("otok", [128, 512]); mixa = sb("mixa", [128, 512], BF16)
        qs = sb("qs", [128, min(1024, T)], BF16); pT = sb("pT", [128, 2, 1024], BF16)
        oas = sb("oas", [65, 1024]); rz = sb("rz", [128, 8]); aos = sb("aos", [128, 8, 64])
        aot = sb("aot", [128, 768]); mixb = sb("mixb", [128, 1024], BF16)
        wflat = wsb[:].rearrange("p c n -> p (c n)")
        KTs = wflat[:, 0:T]
        VAs = wflat[:, T:T + NT * 130].rearrange("p (t c) -> p t c", c=130)
        psA = pst("psA", [128, 2048])
        psB = pst("psB", [128, 2048])

        def rstd_from_ss(ss_ap, n, key_r, key_w):
            S.add("act", lambda e: e.activation(out=ss_ap, in_=ss_ap, func=AF.Ln, scale=1.0 / n, bias=eps_t[:, 0:1]), [key_r, "eps"], [key_w])
            S.add("act", lambda e: e.activation(out=ss_ap, in_=ss_ap, func=AF.Exp, scale=-0.5), [key_w], [key_w])

        def transpose_to(dst_ap, src_ap, ps_ap, rkeys, wkeys, pskey, bf=True, evac="dve"):
            idn = identb if bf else ident
            S.add("pe", lambda e: e.transpose(ps_ap, src_ap, idn[:]), list(rkeys) + ["identb" if bf else "ident"], [pskey])
            S.add(evac, (lambda e: e.tensor_copy(out=dst_ap, in_=ps_ap)) if evac != "act" else (lambda e: e.copy(out=dst_ap, in_=ps_ap)), [pskey], list(wkeys))

        psAb = psA[:].bitcast(BF16) if False else None

        for l in range(DEPTH):
            dma("sp", preB1[:, 0, :], pre_w[l:l + 1, :].partition_broadcast(128), [], ["preB"])
            dma("sp", postB1[:, 0, :], post_w[l:l + 1, :].partition_broadcast(128), [], ["postB"])
            xin = x_in if l == 0 else x1
            xout = x1 if l == 0 else y_out
            xin_key = None
            for c in range(8):
                dma("pool", wsb[:, c, :], w_in_d[l, c * 128:(c + 1) * 128, :], [], [("wsb", c), "KTs", "VAs"])
                dma("pool", wosb[:, c, :], w_out_d[l, c * 128:(c + 1) * 128, :], [], [("wosb", c)])

            for t in range(NT):
                r0 = t * 128
                dma("sp", xt[:], xin[r0:r0 + 128, :], ([("x1", t)] if l == 1 else []), ["xt"])
                S.add("act", lambda e: e.activation(out=sq[:], in_=xt[:], func=AF.Square, accum_out=st[:, 0:1]), ["xt"], ["sq", "st"])
                rstd_from_ss(st[:, 0:1], D, "st", "st")
                S.add("dve", lambda e, l=l: e.scalar_tensor_tensor(out=hb[:], in0=xt[:], scalar=st[:, 0:1], in1=preB1[:, 0, :], op0=ALU.mult, op1=ALU.mult),
                      ["xt", "st", "preB"], ["hb"])
                psT = psA[:, 0:512].bitcast(BF16)
                for c in range(8):
                    S.add("pe", lambda e, c=c: e.transpose(psT[:, c * 128:(c + 1) * 128], hb[:, c * 128:(c + 1) * 128], identb[:]), ["hb", "identb"], [("psA", 0)])
                S.add("dve", lambda e: e.tensor_copy(out=hT[:].rearrange("p c t -> p (c t)"), in_=psT[:, 0:1024]), [("psA", 0)], ["hT"])
                for g in range(9):
                    c0 = g * 512
                    w = min(512, INC - c0)
                    bank = psB[:, (g % 4) * 512:(g % 4) * 512 + w]
                    for c in range(8):
                        S.add("pe", lambda e, c=c, c0=c0, w=w, bank=bank: e.matmul(bank, hT[:, c, :], wsb[:, c, c0:c0 + w], start=(c == 0), stop=(c == 7)),
                              ["hT", ("wsb", c)], [("psB", g % 4)])
                    S.add("act" if g % 2 else "dve",
                          (lambda e, c0=c0, w=w, bank=bank: e.copy(out=pr[:, c0:c0 + w], in_=bank)) if g % 2 else
                          (lambda e, c0=c0, w=w, bank=bank: e.tensor_copy(out=pr[:, c0:c0 + w], in_=bank)),
                          [("psB", g % 4)], [("pr", g)])
                PR = lambda a, b: [("pr", g) for g in range(a // 512, (b - 1) // 512 + 1)]
                def silu(dst, src, n, rk, wk, tmp, tk):
                    S.add("act", lambda e: e.activation(out=tmp, in_=src, func=AF.Exp, scale=-1.0), rk, [tk])
                    S.add("dve", lambda e: e.tensor_scalar_add(out=tmp, in0=tmp, scalar1=1.0), [tk], [tk])
                    S.add("dve", lambda e: e.reciprocal(out=tmp, in_=tmp), [tk], [tk])
                    S.add("dve", lambda e: e.tensor_tensor(out=dst, in0=src, in1=tmp, op=ALU.mult), rk + [tk], [wk])
                silu(tq[:], pr[:, 0:512], 512, PR(0, 512), "tq", tA[:], "tA")
                dma("sp", HQ[r0:r0 + 128, :], tq[:], ["tq"], [("HQ", t)])
                S.add("dve", lambda e, t=t: e.tensor_scalar(out=ropeb[:, 0:512], in0=pr[:, 1536:2048], scalar1=tmask[:, t:t + 1], scalar2=None, op0=ALU.mult),
                      PR(1536, 2048) + ["tmask"], ["rbA"])
                dma("sp", HV[r0:r0 + 128, :], ropeb[:, 0:512], ["rbA"], [("HV", t)])
                silu(gsb[:, 0:512], pr[:, 2048:2560], 512, PR(2048, 2560), "g0", tB[:], "tB")
                silu(gsb[:, 512:768], pr[:, 3072:3328], 256, PR(3072, 3328), "g1", tC[:, 0:256], "tC0")
                silu(gsb[:, 768:1024], pr[:, 4096:4352], 256, PR(4096, 4352), "g2", tC[:, 256:512], "tC1")
                dma("sp", GATE[r0:r0 + 128, :], gsb[:], ["g0", "g1", "g2"], [("GATE", t)])
                for d in range(2):
                    src = pr[:, 512 + d * 512:1024 + d * 512]
                    rk = PR(512 + d * 512, 1024 + d * 512)
                    fk = "tD"
                    S.add("act", lambda e, src=src: e.activation(out=tD[:], in_=src, func=AF.Exp, scale=-1.0), rk, [fk])
                    S.add("dve", lambda e: e.tensor_scalar_add(out=tD[:], in0=tD[:], scalar1=1.0), [fk], [fk])
                    S.add("dve", lambda e: e.reciprocal(out=tD[:], in_=tD[:]), [fk], [fk])
                    if l == 1:
                        S.add("dve", lambda e, d=d: e.tensor_tensor(out=tD[:], in0=tD[:], in1=oml1[:, d, :], op=ALU.mult), [fk, ("oml1", d)], [fk])
                        S.add("dve", lambda e, d=d: e.tensor_tensor(out=tD[:], in0=tD[:], in1=lb1[:, d, :], op=ALU.add), [fk, ("lb1", d)], [fk])
                    S.add("act", lambda e: e.activation(out=tA[:], in_=tD[:], func=AF.Ln), [fk], ["tA"])
                    dma("sp", HG[d, r0:r0 + 128, :], tA[:], ["tA"], [("HG", d, t)])
                    S.add("dve", lambda e: e.tensor_scalar(out=tB[:], in0=tD[:], scalar1=-1.0, scalar2=1.0, op0=ALU.mult, op1=ALU.add), [fk], ["tB"])
                    dma("sp", HK[d, r0:r0 + 128, :], tB[:], ["tB"], [("HK", d, t)])
                dma("sp", cB[:], cosB_d[r0:r0 + 128, :], [], ["cB"]); dma("sp", sB[:], sinB_d[r0:r0 + 128, :], [], ["sB"])
                dma("sp", cC[:], cosC_d[r0:r0 + 128, :], [], ["cC"]); dma("sp", sC[:], sinC_d[r0:r0 + 128, :], [], ["sC"])
                S.add("dve", lambda e: e.tensor_tensor(out=sq[:, 0:384], in0=pr[:, 2560:2944], in1=pr[:, 2560:2944], op=ALU.mult), PR(2560, 2944), ["sq"])
                S.add("dve", lambda e: e.reduce_sum(out=st[:, 1:7], in_=sq[:, 0:384].rearrange("p (h d) -> p h d", d=64), axis=AX.X), ["sq"], ["st"])
                rstd_from_ss(st[:, 1:7], 64, "st", "st")
                S.add("dve", lambda e: e.tensor_tensor(out=ropeq[:, 0:384].rearrange("p (h d) -> p h d", d=64), in0=pr[:, 2560:2944].rearrange("p (h d) -> p h d", d=64),
                                                       in1=st[:, 1:7].unsqueeze(2).broadcast_to([128, 6, 64]), op=ALU.mult), PR(2560, 2944) + ["st"], ["rqB"])
                S.add("dve", lambda e, l=l: e.tensor_tensor(out=ropeq[:, 0:256].rearrange("p (h d) -> p h d", d=64), in0=ropeq[:, 0:256].rearrange("p (h d) -> p h d", d=64),
                                                            in1=qnB[:, l, :].unsqueeze(1).broadcast_to([128, 4, 64]), op=ALU.mult), ["rqB", ("qnB", l)], ["rqB"])
                S.add("dve", lambda e, l=l: e.tensor_tensor(out=ropeq[:, 256:384].rearrange("p (h d) -> p h d", d=64), in0=ropeq[:, 256:384].rearrange("p (h d) -> p h d", d=64),
                                                            in1=knB[:, l, :].unsqueeze(1).broadcast_to([128, 2, 64]), op=ALU.mult), ["rqB", ("knB", l)], ["rqB"])
                S.add("act", lambda e: e.copy(out=ropeq[:, 384:896], in_=pr[:, 3328:3840]), PR(3328, 3840), ["rqC"])
                def rope(x_ap, n_heads_total, cos_t, sin_t, width, rk, wk):
                    nb = width // 32
                    xv = x_ap.rearrange("p (h b two j) -> p (h b) two j", b=nb, two=2, j=16)
                    G = n_heads_total * nb
                    t1 = sq[:, 0:G * 32].rearrange("p (g two j) -> p g two j", two=2, j=16)
                    cv = cos_t.rearrange("p (b two j) -> p b two j", two=2, j=16)
                    sv = sin_t.rearrange("p (b two j) -> p b two j", two=2, j=16)
                    x4 = x_ap.rearrange("p (h b two j) -> p h b two j", b=nb, two=2, j=16)
                    t4 = sq[:, 0:G * 32].rearrange("p (h b two j) -> p h b two j", b=nb, two=2, j=16)
                    for hh in range(2):
                        S.add("dve", lambda e, hh=hh: e.tensor_tensor(out=t4[:, :, :, hh, :], in0=x4[:, :, :, 1 - hh, :],
                                                                      in1=sv[:, :, hh, :].unsqueeze(1).broadcast_to([128, n_heads_total, nb, 16]), op=ALU.mult), rk, ["sq"])
                    S.add("dve", lambda e: e.tensor_tensor(out=x4.rearrange("p h b two j -> p h b (two j)"), in0=x4.rearrange("p h b two j -> p h b (two j)"),
                                                           in1=cos_t.rearrange("p (b w) -> p b w", w=32).unsqueeze(1).broadcast_to([128, n_heads_total, nb, 32]), op=ALU.mult),
                          rk + ["sq", "sq"], [wk])
                    S.add("dve", lambda e: e.tensor_tensor(out=x_ap, in0=x_ap, in1=sq[:, 0:G * 32], op=ALU.add), [wk, "sq", "sq"], [wk])
                rope(ropeq[:, 0:384], 6, cB[:], sB[:], 64, ["rqB", "rqB", "cB", "sB"], "rqB")
                rope(ropeq[:, 384:896], 16, cC[:], sC[:], 32, ["rqC", "cC", "sC"], "rqC")
                qv = ropeq[:, 0:256].rearrange("p (a b d) -> p a b d", a=2, b=2)
                qb_ = ropeb[:, 512:768].rearrange("p (b a d) -> p b a d", b=2, a=2)
                S.add("dve", lambda e: e.tensor_copy(out=qb_.rearrange("p b a d -> p a b d"), in_=qv), ["rqB"], ["rb0"])
                S.add("dve", lambda e: e.tensor_copy(out=ropeb[:, 768:896], in_=ropeq[:, 256:384]), ["rqB"], ["rb1"])
                S.add("act", lambda e: e.copy(out=ropeb[:, 896:1408 - 384], in_=ropeq[:, 384:896 - 384 + 384 - 384 + 128]), ["rqC"], ["rb2x"]) if False else None
                S.add("dve", lambda e: e.tensor_copy(out=ropeb[:, 0:512], in_=ropeq[:, 384:896]), ["rqC"], ["rbA"])
                srcs = [(512, "rb0"), (640, "rb0"), (0, "rbA"), (128, "rbA"), (768, "rb1"), (256, "rbA"), (384, "rbA")]
                psT2 = psA[:, 512:1024].bitcast(BF16)
                for i, (o, k) in enumerate(srcs):
                    S.add("pe", lambda e, i=i, o=o: e.transpose(psT2[:, i * 128:(i + 1) * 128], ropeb[:, o:o + 128], identb[:]), [k, "identb"], [("psA", 1)])
                S.add("act", lambda e: e.copy(out=qkT[:].rearrange("p i t -> p (i t)"), in_=psT2[:, 0:896]), [("psA", 1)], ["qkT"])
                dma("sp", QT[:, :, r0:r0 + 128].rearrange("i p t -> p i t"), qkT[:, 0:4, :], ["qkT"], [("QT", t)])
                dma("sp", KT[:, :, r0:r0 + 128].rearrange("i p t -> p i t"), qkT[:, 4:7, :], ["qkT"], [("KT", t)])
                for gi, c0 in enumerate((2944, 3840, 3968)):
                    S.add("dve", lambda e, gi=gi, c0=c0, t=t: e.tensor_scalar(out=vaug[:, gi, :].rearrange("p (h d) -> p h d", d=65)[:, :, 0:64],
                                                                            in0=pr[:, c0:c0 + 128].rearrange("p (h d) -> p h d", d=64), scalar1=tmask[:, t:t + 1], scalar2=None, op0=ALU.mult),
                          PR(c0, c0 + 128) + ["tmask"], [("vaug", gi)])
                    S.add("dve", lambda e, gi=gi, t=t: e.tensor_copy(out=vaug[:, gi, :].rearrange("p (h d) -> p h d", d=65)[:, :, 64:65],
                                                                    in_=tmask[:, t:t + 1].unsqueeze(1).broadcast_to([128, 2, 1])), ["tmask"], [("vaug1", gi)])
                dma("sp", VA[:, :, t, :].rearrange("g p c -> p g c"), vaug[:], [("vaug", g) for g in range(3)] + [("vaug1", g) for g in range(3)], [("VA", t)])


            for d in range(2):
                S.add("dve", lambda e: e.memset(S32[:], 0.0), [], ["S32"])
                S.add("dve", lambda e: e.memset(S16[:], 0.0), [], ["S16"])
                order = range(NT) if d == 0 else range(NT - 1, -1, -1)
                for t in order:
                    r0 = t * 128
                    dma("sp", hq[:], HQ[r0:r0 + 128, :], [("HQ", t)], ["hq"])
                    dma("sp", hv[:], HV[r0:r0 + 128, :], [("HV", t)], ["hv"])
                    dma("sp", hg[:], HG[d, r0:r0 + 128, :], [("HG", d, t)], ["hg"])
                    dma("sp", hk[:], HK[d, r0:r0 + 128, :], [("HK", d, t)], ["hk"])
                    for i in range(3):
                        S.add("pe", lambda e, i=i, d=d: e.matmul(psA[:, i * 512:(i + 1) * 512], hm[:, 3 * d + i, :], hg[:], start=True, stop=True), ["hm", "hg"], [("psA", i)])
                    S.add("act", lambda e: e.activation(out=tC[:], in_=psA[:, 0:512], func=AF.Exp), [("psA", 0)], ["tC0", "tC1"])
                    S.add("act", lambda e: e.activation(out=tA[:], in_=psA[:, 512:1024], func=AF.Exp), [("psA", 1)], ["tA"])
                    S.add("act", lambda e: e.activation(out=tB[:], in_=psA[:, 512:1024], func=AF.Exp, scale=-1.0), [("psA", 1)], ["tB"])
                    S.add("act", lambda e: e.activation(out=tD[:], in_=psA[:, 1024:1536], func=AF.Exp), [("psA", 2)], ["tD"])
                    S.add("dve", lambda e: e.tensor_tensor(out=qmb[:], in0=hq[:], in1=tA[:], op=ALU.mult), ["hq", "tA"], ["qmb"])
                    S.add("dve", lambda e: e.tensor_tensor(out=kmb[:], in0=hk[:], in1=tB[:], op=ALU.mult), ["hk", "tB"], ["kmb"])
                    S.add("dve", lambda e: e.tensor_tensor(out=qbb[:], in0=hq[:], in1=tC[:], op=ALU.mult), ["hq", "tC0", "tC1"], ["qbb"])
                    S.add("dve", lambda e: e.tensor_tensor(out=tD[:], in0=hk[:], in1=tD[:], op=ALU.mult), ["hk", "tD"], ["tD"])
                    for c in range(4):
                        S.add("dve", lambda e, c=c: e.tensor_scalar(out=kdm[:, c, :], in0=tD[:], scalar1=sel[:, c:c + 1], scalar2=None, op0=ALU.mult), ["tD", "sel"], [("kdm", c)])
                    for h in range(4):
                        S.add("pe", lambda e, h=h: e.matmul(psA[:, 1536 + h * 4:1536 + h * 4 + 4], hg[:, h * 128:(h + 1) * 128], sel[:], start=True, stop=True), ["hg", "sel"], [("psA", 3)])
                    S.add("act", lambda e: e.activation(out=dec[:], in_=psA[:, 1536:1552], func=AF.Exp), [("psA", 3)], ["dec"])
                    pb0 = psB[:, 0:512].bitcast(BF16)
                    pb1 = psB[:, 512:1024].bitcast(BF16)
                    for h in range(4):
                        S.add("pe", lambda e, h=h: e.transpose(pb0[:, h * 128:(h + 1) * 128], qmb[:, h * 128:(h + 1) * 128], identb[:]), ["qmb", "identb"], [("psB", 0)])
                        S.add("pe", lambda e, h=h: e.transpose(pb0[:, 512 + h * 128:512 + (h + 1) * 128], kmb[:, h * 128:(h + 1) * 128], identb[:]), ["kmb", "identb"], [("psB", 0)])
                        S.add("pe", lambda e, h=h: e.transpose(pb1[:, h * 128:(h + 1) * 128], qbb[:, h * 128:(h + 1) * 128], identb[:]), ["qbb", "identb"], [("psB", 1)])
                    S.add("dve", lambda e: e.tensor_copy(out=qkmT[:], in_=pb0[:, 0:1024]), [("psB", 0)], ["qkmT"])
                    S.add("act", lambda e: e.copy(out=qbT[:], in_=pb1[:, 0:512]), [("psB", 1)], ["qbT"])
                    for h in range(4):
                        S.add("pe", lambda e, h=h: e.matmul(psB[:, 1024 + h * 128:1024 + (h + 1) * 128], qkmT[:, 512 + h * 128:512 + (h + 1) * 128], qkmT[:, h * 128:(h + 1) * 128], start=True, stop=True),
                              ["qkmT"], [("psB", 2)])
                    S.add("dve", lambda e, d=d: e.tensor_tensor(out=ATb[:].rearrange("p (h t) -> p h t", h=4), in0=psB[:, 1024:1536].rearrange("p (h t) -> p h t", h=4),
                                                                in1=am[:, d, :].unsqueeze(1).broadcast_to([128, 4, 128]), op=ALU.mult), [("psB", 2), "am"], ["ATb"])
                    corder = range(4) if d == 0 else range(3, -1, -1)
                    for ci, c in enumerate(corder):
                        bk = 0 if ci % 2 == 0 else 1
                        for h in range(4):
                            S.add("pe", lambda e, h=h, c=c, bk=bk: e.matmul(psA[:, bk * 512 + h * 128:bk * 512 + (h + 1) * 128], kdm[:, c, h * 128:(h + 1) * 128], hv[:, h * 128:(h + 1) * 128], start=True, stop=True),
                                  [("kdm", c), "hv"], [("psA", bk)])
                        for h in range(4):
                            o_ap = psB[:, 1536 + h * 128 + c * 32:1536 + h * 128 + (c + 1) * 32]
                            S.add("pe", lambda e, h=h, c=c, o_ap=o_ap: e.matmul(o_ap, hv[:, h * 128:(h + 1) * 128], ATb[:, h * 128 + c * 32:h * 128 + (c + 1) * 32], start=True, stop=False),
                                  ["hv", "ATb"], [("psB", 3)])
                            S.add("pe", lambda e, h=h, c=c, o_ap=o_ap: e.matmul(o_ap, S16[:, h * 128:(h + 1) * 128], qbT[:, h * 128 + c * 32:h * 128 + (c + 1) * 32], start=False, stop=True),
                                  ["S16", "qbT"], [("psB", 3)])
                        for h in range(4):
                            S.add("dve", lambda e, h=h, c=c, bk=bk: e.scalar_tensor_tensor(out=S32[:, h * 128:(h + 1) * 128], in0=S32[:, h * 128:(h + 1) * 128], scalar=dec[:, h * 4 + c:h * 4 + c + 1],
                                                                                         in1=psA[:, bk * 512 + h * 128:bk * 512 + (h + 1) * 128], op0=ALU.mult, op1=ALU.add),
                                  ["S32", "dec", ("psA", bk)], ["S32"])
                        S.add("act", lambda e: e.copy(out=S16[:], in_=S32[:]), ["S32"], ["S16"])
                    S.add("dve", lambda e: e.tensor_copy(out=oTs[:], in_=psB[:, 1536:2048]), [("psB", 3)], ["oTs"])
                    for h in range(4):
                        S.add("pe", lambda e, h=h: e.transpose(psA[:, 1024 + h * 128:1024 + (h + 1) * 128], oTs[:, h * 128:(h + 1) * 128], ident[:]), ["oTs", "ident"], [("psA", 2)])
                    if d == 0:
                        S.add("act", lambda e: e.copy(out=otok[:], in_=psA[:, 1024:1536]), [("psA", 2)], ["otok"])
                        dma("sp", OH[r0:r0 + 128, :], otok[:], ["otok"], [("OH", t)])
                    else:
                        dma("sp", otok[:], OH[r0:r0 + 128, :], [("OH", t)], ["otok"])
                        dma("sp", gsb[:, 0:512], GATE[r0:r0 + 128, 0:512], [("GATE", t)], ["g0"])
                        S.add("dve", lambda e: e.tensor_tensor(out=otok[:], in0=otok[:], in1=psA[:, 1024:1536], op=ALU.add), ["otok", ("psA", 2)], ["otok"])
                        S.add("dve", lambda e: e.tensor_tensor(out=sq[:, 0:512], in0=otok[:], in1=otok[:], op=ALU.mult), ["otok"], ["sq"])
                        S.add("dve", lambda e: e.reduce_sum(out=st[:, 0:4], in_=sq[:, 0:512].rearrange("p (h d) -> p h d", h=4), axis=AX.X), ["sq"], ["st"])
                        rstd_from_ss(st[:, 0:4], 128, "st", "st")
                        S.add("dve", lambda e: e.tensor_tensor(out=otok[:].rearrange("p (h d) -> p h d", h=4), in0=otok[:].rearrange("p (h d) -> p h d", h=4),
                                                               in1=st[:, 0:4].unsqueeze(2).broadcast_to([128, 4, 128]), op=ALU.mult), ["otok", "st"], ["otok"])
                        S.add("dve", lambda e, l=l: e.tensor_tensor(out=otok[:], in0=otok[:], in1=hnB[:, l, :], op=ALU.mult), ["otok", ("hnB", l)], ["otok"])
                        S.add("dve", lambda e: e.tensor_tensor(out=mixa[:], in0=otok[:], in1=gsb[:, 0:512], op=ALU.mult), ["otok", "g0"], ["mixa"])
                        dma("sp", MIX[r0:r0 + 128, 0:512], mixa[:], ["mixa"], [("MIXA", t)])

            QJ = min(1024, T)
            NQG = QJ // 512 if QJ >= 512 else 1
            QW = min(512, QJ)
            for gi in range(3):
                dma("sp", KTs, KT[gi, :, :], [("KT", t) for t in range(NT)], ["KTs"] + [("wsb", c) for c in range(8)])
                dma("sp", VAs, VA[gi, :, :, :], [("VA", t) for t in range(NT)], ["VAs"] + [("wsb", c) for c in range(8)])
                if gi == 0:
                    units = [(b, a * 64, 64, a, a * 2 + b) for b in range(2) for a in range(2)]
                else:
                    units = [(2 + gi - 1, hh * 64 + m * 32, 32, hh, 4 + (2 * (gi - 1) + hh) * 2 + m) for hh in range(2) for m in range(2)]
                for (qt, base, dk, vh, slot) in units:
                    scale = 1.0 / math.sqrt(dk)
                    for j in range(T // QJ):
                        q0 = j * QJ
                        dma("sp", qs[:], QT[qt, :, q0:q0 + QJ], [("QT", t) for t in range(NT)], ["qs"])
                        for kk_ in range(NT + 1):
                            if kk_ < NT:
                                kt = kk_
                                sb_ = kt % 2
                                for qg in range(NQG):
                                    S.add("pe", lambda e, kt=kt, qg=qg, sb_=sb_, base=base, dk=dk: e.matmul(
                                        psA[:, (sb_ * 2 + qg) * 512:(sb_ * 2 + qg) * 512 + QW], KTs[base:base + dk, kt * 128:(kt + 1) * 128], qs[base:base + dk, qg * QW:(qg + 1) * QW],
                                        start=True, stop=True, tile_position=((96, 0) if base == 96 else None)), ["KTs", "qs"], [("psA", 2 * sb_), ("psA", 2 * sb_ + 1)])
                                S.add("act", lambda e, sb_=sb_, scale=scale: e.activation(out=pT[:, sb_, 0:QJ], in_=psA[:, sb_ * 1024:sb_ * 1024 + QJ], func=AF.Exp, scale=scale), [("psA", 2 * sb_), ("psA", 2 * sb_ + 1)], [("pT", sb_)])
                            if kk_ >= 1:
                                kt = kk_ - 1
                                sb_ = kt % 2
                                for qg in range(NQG):
                                    S.add("pe", lambda e, kt=kt, qg=qg, sb_=sb_, vh=vh: e.matmul(psB[0:65, qg * 512:qg * 512 + QW], VAs[:, kt, vh * 65:(vh + 1) * 65], pT[:, sb_, qg * QW:(qg + 1) * QW],
                                                                                              start=(kt == 0), stop=(kt == NT - 1)), ["VAs", ("pT", sb_)], [("psB", 0), ("psB", 1)])
                        S.add("dve", lambda e: e.tensor_copy(out=oas[:, 0:QJ], in_=psB[0:65, 0:QJ]), [("psB", 0), ("psB", 1)], ["oas"])
                        for jb in range(QJ // 128):
                            S.add("pe", lambda e, jb=jb: e.transpose(psB[:, 1024 + jb * 128:1024 + jb * 128 + 65], oas[:, jb * 128:(jb + 1) * 128], ident[0:65, 0:65]), ["oas", "ident"], [("psB", 2), ("psB", 3)])
                        nb_ = QJ // 128
                        S.add("dve", lambda e, nb_=nb_: e.reciprocal(out=rz[:, 0:nb_], in_=psB[:, 1024:1024 + nb_ * 128].rearrange("p (j c) -> p j c", c=128)[:, :, 64]), [("psB", 2), ("psB", 3)], ["rz"])
                        S.add("dve", lambda e, nb_=nb_: e.tensor_tensor(out=aos[:, 0:nb_, :], in0=psB[:, 1024:1024 + nb_ * 128].rearrange("p (j c) -> p j c", c=128)[:, :, 0:64],
                                                                        in1=rz[:, 0:nb_].unsqueeze(2).broadcast_to([128, nb_, 64]), op=ALU.mult), [("psB", 2), ("psB", 3), "rz"], ["aos"])
                        dma("sp", AO[q0:q0 + QJ, slot, :].rearrange("(j p) d -> p j d", p=128), aos[:, 0:nb_, :], ["aos"], [("AO", slot, j)])

            lam_init = 0.8 - 0.6 * math.exp(-0.3 * l)
            for t in range(NT):
                r0 = t * 128
                dma("sp", mixb[:, 0:512], MIX[r0:r0 + 128, 0:512], [("MIXA", t)], ["mixb0"])
                dma("sp", aot[:], AO[r0:r0 + 128, :, :].rearrange("p s d -> p (s d)"), [("AO", s_, j_) for s_ in range(12) for j_ in range(T // QJ)], ["aot"])
                dma("sp", gsb[:], GATE[r0:r0 + 128, :], [("GATE", t)], ["g0", "g1", "g2"])
                dma("sp", xt[:], xin[r0:r0 + 128, :], ([("x1", t)] if l == 1 else []), ["xt"])
                S.add("dve", lambda e: e.tensor_tensor(out=mixb[:, 512:768], in0=aot[:, 0:256], in1=gsb[:, 512:768], op=ALU.mult), ["aot", "g1"], ["mixb1"])
                av = aot[:, 256:768].rearrange("p (h m d) -> p h m d", h=4, m=2)
                S.add("dve", lambda e, l=l, av=av: e.scalar_tensor_tensor(out=tA[:, 0:256].rearrange("p (h d) -> p h d", h=4), in0=av[:, :, 1, :], scalar=lamv[:, l, 3:4], in1=av[:, :, 0, :],
                                                                        op0=ALU.mult, op1=ALU.add), ["aot", ("nlam", l)], ["tA"])
                S.add("dve", lambda e: e.tensor_tensor(out=sq[:, 0:256], in0=tA[:, 0:256], in1=tA[:, 0:256], op=ALU.mult), ["tA"], ["sq"])
                S.add("dve", lambda e: e.reduce_sum(out=st[:, 0:4], in_=sq[:, 0:256].rearrange("p (h d) -> p h d", h=4), axis=AX.X), ["sq"], ["st"])
                rstd_from_ss(st[:, 0:4], 64, "st", "st")
                S.add("dve", lambda e: e.tensor_tensor(out=tA[:, 0:256].rearrange("p (h d) -> p h d", h=4), in0=tA[:, 0:256].rearrange("p (h d) -> p h d", h=4),
                                                       in1=st[:, 0:4].unsqueeze(2).broadcast_to([128, 4, 64]), op=ALU.mult), ["tA", "st"], ["tA"])
                S.add("dve", lambda e, l=l: e.tensor_tensor(out=tA[:, 0:256].rearrange("p (h d) -> p h d", h=4), in0=tA[:, 0:256].rearrange("p (h d) -> p h d", h=4),
                                                            in1=dnB[:, l, :].unsqueeze(1).broadcast_to([128, 4, 64]), op=ALU.mult), ["tA", ("dnB", l)], ["tA"])
                S.add("dve", lambda e, li=lam_init: e.scalar_tensor_tensor(out=mixb[:, 768:1024], in0=tA[:, 0:256], scalar=1.0 - li, in1=gsb[:, 768:1024], op0=ALU.mult, op1=ALU.mult), ["tA", "g2"], ["mixb2"])
                pm = psA[:, 0:512].bitcast(BF16)
                for c in range(8):
                    S.add("pe", lambda e, c=c: e.transpose(pm[:, c * 128:(c + 1) * 128], mixb[:, c * 128:(c + 1) * 128], identb[:]), ["mixb0", "mixb1", "mixb2", "identb"], [("psA", 0)])
                S.add("dve", lambda e: e.tensor_copy(out=hT[:].rearrange("p c t -> p (c t)"), in_=pm[:, 0:1024]), [("psA", 0)], ["hT"])
                for n in range(2):
                    for c in range(8):
                        S.add("pe", lambda e, c=c, n=n: e.matmul(psB[:, n * 512:(n + 1) * 512], hT[:, c, :], wosb[:, c, n * 512:(n + 1) * 512], start=(c == 0), stop=(c == 7)),
                              ["hT", ("wosb", c)], [("psB", n)])
                S.add("act", lambda e: e.activation(out=sq[:], in_=psB[:, 0:1024], func=AF.Square, accum_out=st[:, 4:5]), [("psB", 0), ("psB", 1)], ["sq", "st"])
                rstd_from_ss(st[:, 4:5], D, "st", "st")
                S.add("dve", lambda e, l=l: e.scalar_tensor_tensor(out=pr[:, 0:1024], in0=psB[:, 0:1024], scalar=st[:, 4:5], in1=postB1[:, 0, :], op0=ALU.mult, op1=ALU.mult),
                      [("psB", 0), ("psB", 1), "st", "postB"], [("pr", 0), ("pr", 1)])
                S.add("dve", lambda e: e.tensor_tensor(out=pr[:, 0:1024], in0=pr[:, 0:1024], in1=xt[:], op=ALU.add), [("pr", 0), ("pr", 1), "xt"], [("pr", 0), ("pr", 1)])
                dma("sp", xout[r0:r0 + 128, :], pr[:, 0:1024], [("pr", 0), ("pr", 1)], [("x1", t) if l == 0 else ("yout", t)], is_out=(l == 1))

        S.emit(nc, stack)
    return nc


def kernel(**inputs):
    T = 16384
    NT = T // 128
    xp = np.asarray(inputs["x_prompt"], np.float32)
    xs = np.asarray(inputs["x_sample"], np.float32)
    seqs = [xp[0]] + [xs[b] for b in range(xs.shape[0])]
    while len(seqs) < NCORES:
        seqs.append(xs[0])
    consts = make_consts(T)
    shared = {k: np.ascontiguousarray(np.asarray(inputs[k], np.float32)) for k in
              ("w_in", "w_out", "pre_norm_w", "post_norm_w", "hgrn_lb", "hgrn_norm_w", "gqa_q_norm_w",
               "gqa_k_norm_w", "diff_lambda", "diff_norm_w")}
    in_maps = []
    for c in range(NCORES):
        sq_ = seqs[c]
        L = sq_.shape[0]
        xpad = np.zeros((T, D), np.float32)
        xpad[:L] = sq_
        idx = np.arange(128)[:, None] + 128 * np.arange(NT)[None, :]
        tmask = (idx < L).astype(np.float32)
        m = dict(x=xpad, tmask=tmask)
        m.update(shared)
        m.update(consts)
        in_maps.append(m)
    nc = build(T)
    res = run_bass_kernel_spmd(nc, in_maps, core_ids=list(range(NCORES)))
    r = res.results
    y_prompt = np.asarray(r[0]["y"], np.float32)[None, :xp.shape[1], :]
    y_sample = np.stack([np.asarray(r[1 + b]["y"], np.float32)[:xs.shape[1]] for b in range(xs.shape[0])], 0)
    return (np.ascontiguousarray(y_prompt), np.ascontiguousarray(y_sample))
```
